# Optimizing a Trainium2 kernel written in Bass

```python
import math
import jax, jax.numpy as jnp
from jax import lax
import numpy as np

D_MODEL = 2048
BATCH = 32
SEQ = 256
DEPTH = 2
DEC_BATCH = 2
DEC_SEQ = 4096
PAST_LEN = 512

GRID_W = 64
WA = 2048
CONV_A_W = 3
SSD_HEADS = 64
SSD_HEAD_DIM = 64
WB = SSD_HEADS * SSD_HEAD_DIM
SSD_GROUPS = 8
SSD_HPG = SSD_HEADS // SSD_GROUPS
SSD_STATE = 128
SSD_CONV_W = 5
SSD_CHUNK = 128
XBC_W = WB + 2 * SSD_GROUPS * SSD_STATE
WC = 2048
CONF_CONV_W = 31
SPLIT_SIZES = (WA, WA, WA, WA, WB, XBC_W, 2 * SSD_HEADS, 2 * WC, WC, 3 * D_MODEL)
N_IN = sum(SPLIT_SIZES)
SPLIT_POINTS = tuple(int(v) for v in np.cumsum(SPLIT_SIZES)[:-1])
EPS = 1e-6

kernel_name = 'hybrid_gatedconv_ssd_conformer_flow_step'


def rmsnorm(x, g):
    xf = x.astype(jnp.float32)
    y = xf * lax.rsqrt(jnp.mean(xf * xf, axis=-1, keepdims=True) + EPS)
    return y.astype(x.dtype) * g


def layernorm(x, g, b):
    xf = x.astype(jnp.float32)
    mu = jnp.mean(xf, axis=-1, keepdims=True)
    var = jnp.mean(jnp.square(xf - mu), axis=-1, keepdims=True)
    return ((xf - mu) * lax.rsqrt(var + EPS)).astype(x.dtype) * g + b


def dwconv(x, w, b, rows):
    n, L, C = x.shape
    if rows is not None:
        x = x.reshape(n * rows, L // rows, C)
    k = w.shape[0]
    y = lax.conv_general_dilated(x, w[:, None, :], window_strides=(1,),
                                 padding=[(k // 2, k // 2)],
                                 dimension_numbers=('NWC', 'WIO', 'NWC'),
                                 feature_group_count=C)
    if b is not None:
        y = y + b
    return y.reshape(n, L, C)


def ssd_chunked(x, dt, A, bm, cm, h0):
    b, L = x.shape[:2]
    nc = L // SSD_CHUNK
    x = x.reshape(b, nc, SSD_CHUNK, SSD_GROUPS, SSD_HPG, SSD_HEAD_DIM)
    dt = dt.reshape(b, nc, SSD_CHUNK, SSD_GROUPS, SSD_HPG)
    bm = bm.reshape(b, nc, SSD_CHUNK, SSD_GROUPS, SSD_STATE)
    cm = cm.reshape(b, nc, SSD_CHUNK, SSD_GROUPS, SSD_STATE)
    a_cum = jnp.cumsum(jnp.moveaxis(dt * A, 2, -1), axis=-1)
    xdt = x * dt[..., None]
    tril = jnp.tril(jnp.ones((SSD_CHUNK, SSD_CHUNK), dtype=bool))
    decay = jnp.exp(jnp.where(tril, a_cum[..., :, None] - a_cum[..., None, :], -jnp.inf))
    scores = jnp.einsum('bcign,bcjgn->bcgij', cm, bm)
    y_diag = jnp.einsum('bcgij,bcgkij,bcjgkp->bcigkp', scores, decay, xdt)
    decay_end = jnp.exp(a_cum[..., -1:] - a_cum)
    states = jnp.einsum('bcjgn,bcgkj,bcjgkp->bcgkpn', bm, decay_end, xdt)
    chunk_decay = jnp.exp(a_cum[..., -1])

    def step(h, inp):
        s, d = inp
        return h * d[..., None, None] + s, h

    h_last, h_prev = lax.scan(step, h0.astype(jnp.float32),
                              (jnp.moveaxis(states, 1, 0), jnp.moveaxis(chunk_decay, 1, 0)))
    h_prev = jnp.moveaxis(h_prev, 0, 1)
    y_off = jnp.einsum('bcign,bcgkpn,bcgki->bcigkp', cm, h_prev, jnp.exp(a_cum))
    y = (y_diag + y_off).reshape(b, L, SSD_GROUPS, SSD_HPG, SSD_HEAD_DIM)
    return y, h_last


def ssd_bidir(xs, dt_raw, bm, cm, dt_bias, a_log, d_skip, h0_f, h0_b):
    b, L = xs.shape[:2]
    x = xs.reshape(b, L, SSD_GROUPS, SSD_HPG, SSD_HEAD_DIM)
    bm = bm.reshape(b, L, SSD_GROUPS, SSD_STATE)
    cm = cm.reshape(b, L, SSD_GROUPS, SSD_STATE)
    dt = jax.nn.softplus(dt_raw.astype(jnp.float32).reshape(b, L, 2, SSD_GROUPS, SSD_HPG)
                         + dt_bias.astype(jnp.float32).reshape(2, SSD_GROUPS, SSD_HPG))
    A = -jnp.exp(a_log.astype(jnp.float32)).reshape(2, SSD_GROUPS, SSD_HPG)
    y_f, h_f = ssd_chunked(x, dt[:, :, 0], A[0], bm, cm, h0_f)
    flip = lambda t: jnp.flip(t, axis=1)
    y_b, h_b = ssd_chunked(flip(x), flip(dt[:, :, 1]), A[1], flip(bm), flip(cm), h0_b)
    y = y_f + flip(y_b) + x * d_skip.reshape(SSD_GROUPS, SSD_HPG, 1)
    return y.reshape(b, L, WB).astype(xs.dtype), h_f, h_b


def mixer_layer(x, mod, rows, h0_f, h0_b, g_pre, g_post, w_in, conv_a_w, ssd_conv_w, ssd_conv_b,
                dt_bias, a_log, d_skip, ssd_norm_g, conf_conv_w, conf_conv_b, conf_ln_g, conf_ln_b,
                w_pa, w_pb, w_pc, w_o):
    shift, scale, gate = jnp.split(mod, 3, axis=-1)
    u = rmsnorm(x, g_pre) * (1 + scale) + shift
    proj = jnp.einsum('bld,dn->bln', u, w_in)
    a_b, a_c, a_h, a_z, b_z, b_xbc, b_dt, c_glu, c_z, g_br = jnp.split(proj, SPLIT_POINTS, axis=-1)
    ya = a_b * dwconv(a_c * a_h, conv_a_w, None, rows)
    pa = (ya * jax.nn.silu(a_z)) @ w_pa
    xbc = jax.nn.silu(dwconv(b_xbc, ssd_conv_w, ssd_conv_b, rows))
    xs, bm, cm = jnp.split(xbc, [WB, WB + SSD_GROUPS * SSD_STATE], axis=-1)
    yb, h_f, h_b = ssd_bidir(xs, b_dt, bm, cm, dt_bias, a_log, d_skip, h0_f, h0_b)
    pb = rmsnorm(yb * jax.nn.silu(b_z), ssd_norm_g) @ w_pb
    c_a, c_g = jnp.split(c_glu, 2, axis=-1)
    yc = dwconv(c_a * jax.nn.sigmoid(c_g), conf_conv_w, conf_conv_b, rows)
    yc = jax.nn.silu(layernorm(yc, conf_ln_g, conf_ln_b))
    pc = (yc * jax.nn.silu(c_z)) @ w_pc
    ga, gb, gc = jnp.split(jax.nn.sigmoid(g_br), 3, axis=-1)
    m = (ga * pa + gb * pb + gc * pc) @ w_o
    return x + gate * rmsnorm(m, g_post), h_f, h_b


def setup_inputs(seed: int = 0) -> dict:
    key = jax.random.key(seed)
    ks = jax.random.split(key, 32)
    nrm = lambda k, shape, s: jax.random.normal(k, shape, jnp.float32) * s
    dt0 = jnp.exp(jax.random.uniform(ks[10], (DEPTH, 2, SSD_HEADS), jnp.float32,
                                     minval=math.log(1e-3), maxval=math.log(1e-1)))
    dt_bias = dt0 + jnp.log(-jnp.expm1(-dt0))
    a_log = jnp.log(jax.random.uniform(ks[11], (DEPTH, 2, SSD_HEADS), jnp.float32, minval=1.0, maxval=16.0))
    return {
        'x_prompt': nrm(ks[0], (BATCH, SEQ, D_MODEL), 1.0),
        'x_sample': nrm(ks[1], (DEC_BATCH, DEC_SEQ, D_MODEL), 1.0),
        'state_ssd': nrm(ks[2], (DEC_BATCH, DEPTH, 2, SSD_HEADS, SSD_HEAD_DIM, SSD_STATE), 0.1),
        'c': nrm(ks[3], (DEC_BATCH, D_MODEL), 1.0),
        'c_ctx': nrm(ks[4], (D_MODEL,), 1.0),
        'w_mod': nrm(ks[5], (DEPTH, D_MODEL, 3 * D_MODEL), 0.5 * D_MODEL ** -0.5),
        'b_mod': nrm(ks[6], (DEPTH, 3 * D_MODEL), 0.02),
        'g_pre': 1.0 + nrm(ks[7], (DEPTH, D_MODEL), 0.02),
        'g_post': 1.0 + nrm(ks[8], (DEPTH, D_MODEL), 0.02),
        'w_in': nrm(ks[9], (DEPTH, D_MODEL, N_IN), D_MODEL ** -0.5),
        'conv_a_w': nrm(ks[12], (DEPTH, CONV_A_W, WA), CONV_A_W ** -0.5),
        'ssd_conv_w': nrm(ks[13], (DEPTH, SSD_CONV_W, XBC_W), SSD_CONV_W ** -0.5),
        'ssd_conv_b': nrm(ks[14], (DEPTH, XBC_W), 0.02),
        'dt_bias': dt_bias,
        'a_log': a_log,
        'd_skip': 1.0 + nrm(ks[15], (DEPTH, SSD_HEADS), 0.1),
        'ssd_norm_g': 1.0 + nrm(ks[16], (DEPTH, WB), 0.02),
        'conf_conv_w': nrm(ks[17], (DEPTH, CONF_CONV_W, WC), CONF_CONV_W ** -0.5),
        'conf_conv_b': nrm(ks[18], (DEPTH, WC), 0.02),
        'conf_ln_g': 1.0 + nrm(ks[19], (DEPTH, WC), 0.02),
        'conf_ln_b': nrm(ks[20], (DEPTH, WC), 0.02),
        'w_pa': nrm(ks[21], (DEPTH, WA, D_MODEL), WA ** -0.5),
        'w_pb': nrm(ks[22], (DEPTH, WB, D_MODEL), WB ** -0.5),
        'w_pc': nrm(ks[23], (DEPTH, WC, D_MODEL), WC ** -0.5),
        'w_o': nrm(ks[24], (DEPTH, D_MODEL, D_MODEL), D_MODEL ** -0.5),
    }


def reference(x_prompt, x_sample, state_ssd, c, c_ctx, w_mod, b_mod, g_pre, g_post, w_in, conv_a_w,
              ssd_conv_w, ssd_conv_b, dt_bias, a_log, d_skip, ssd_norm_g, conf_conv_w, conf_conv_b,
              conf_ln_g, conf_ln_b, w_pa, w_pb, w_pc, w_o):
    nb_p = x_prompt.shape[0]
    nb_s = x_sample.shape[0]
    rows = x_sample.shape[1] // GRID_W
    hshape = (SSD_GROUPS, SSD_HPG, SSD_HEAD_DIM, SSD_STATE)
    h_zero = jnp.zeros((nb_p,) + hshape, jnp.float32)
    yp, ys = x_prompt, x_sample
    new_states = []
    for l in range(DEPTH):
        lw = (g_pre[l], g_post[l], w_in[l], conv_a_w[l], ssd_conv_w[l], ssd_conv_b[l], dt_bias[l],
              a_log[l], d_skip[l], ssd_norm_g[l], conf_conv_w[l], conf_conv_b[l], conf_ln_g[l],
              conf_ln_b[l], w_pa[l], w_pb[l], w_pc[l], w_o[l])
        mod_ctx = (jax.nn.silu(c_ctx) @ w_mod[l] + b_mod[l])[None, None, :]
        yp, h_f, h_b = mixer_layer(yp, mod_ctx, None, h_zero, h_zero, *lw)
        new_states.append(jnp.stack([h_f, h_b], axis=1).reshape(nb_p, 2, SSD_HEADS, SSD_HEAD_DIM, SSD_STATE))
        mod_lat = (jax.nn.silu(c) @ w_mod[l] + b_mod[l])[:, None, :]
        h0_f = state_ssd[:, l, 0].reshape((nb_s,) + hshape)
        h0_b = state_ssd[:, l, 1].reshape((nb_s,) + hshape)
        ys, _, _ = mixer_layer(ys, mod_lat, rows, h0_f, h0_b, *lw)
    new_state_ssd = jnp.stack(new_states, axis=1).astype(x_prompt.dtype)
    return (yp, ys, new_state_ssd)
```

```python
import numpy as np
import concourse.bass as bass
import concourse.mybir as mybir
from concourse.bass_utils import run_bass_kernel_spmd

F32 = mybir.dt.float32
BF16 = mybir.dt.bfloat16
AF = mybir.ActivationFunctionType
ALU = mybir.AluOpType

D = 2048
NL = 2
T = 512
NB = 4
KC = 16
H = 64
G = 8
EPS = 1e-6
A_B, A_C, A_H, A_Z = 0, 2048, 4096, 6144
B_Z = 8192
B_X = 12288
B_BM = 16384
B_CM = 17408
B_DT = 18432
C_A = 18560
C_G = 20608
C_Z = 22656
G_A, G_B, G_C = 24704, 26752, 28800
N_IN = 30848

V_GPRE, V_GPOST, V_CCB, V_LNG, V_LNB = 0, 16, 32, 48, 64
V_SNG = 80
V_SCB = 112
V_BMOD = 160
V_CAW = 208
V_SCW = 256
V_CCW = 496
NV = 992
NCST = 6


class Sched:
    def __init__(self, nc):
        self.nc = nc
        self.eng = {"pe": nc.tensor, "act": nc.scalar, "dve": nc.vector, "pool": nc.gpsimd, "sp": nc.sync}
        self.sem = {e: nc.alloc_semaphore(name="sem_" + e) for e in ("pe", "act", "dve", "pool")}
        self.cnt = {e: 0 for e in self.sem}
        self.seen = {e: {} for e in self.eng}
        self.ndma = 24
        self.dsem = [nc.alloc_semaphore(name="dsem%d" % i) for i in range(self.ndma)]
        self.dcnt = [0] * self.ndma
        self.drr = 0
        self.lastw = {}
        self.readers = {}
        self.nops = 0
        self.alias = {}
        self.fence = None
        self.psum_names = set()

    def key(self, ap):
        if isinstance(ap, str):
            return [ap]
        name = ap.tensor.name if hasattr(ap, "tensor") else ap.name
        return self.alias.get(name, [name])

    def _deps(self, reads, writes):
        deps = []
        for k in reads:
            if k in self.lastw:
                deps.append(self.lastw[k])
        for k in writes:
            if k in self.lastw:
                deps.append(self.lastw[k])
            for (src, val) in self.readers.get(k, ()):
                if src == "pe":
                    val = min(val + 2, self.cnt["pe"])
                deps.append((src, val))
        return deps

    paranoid = False

    def _wait(self, e, deps):
        if self.paranoid is True or (self.paranoid and e in self.paranoid):
            deps = list(deps) + [(x, self.cnt[x]) for x in ("pe", "act", "dve") if self.cnt[x] > 0]
            deps += [(s, self.dcnt[s]) for s in range(self.ndma) if self.dcnt[s] > 0]
        need = {}
        for src, val in deps:
            if src == e and e == "pe" and not self.paranoid:
                continue
            if need.get(src, 0) < val:
                need[src] = val
        for src, val in need.items():
            if self.seen[e].get(src, 0) >= val:
                continue
            if src == e and self.cnt[e] - val >= 2:
                continue
            sem = self.sem[src] if isinstance(src, str) else self.dsem[src]
            self.eng[e].wait_ge(sem, val)
            self.seen[e][src] = val

    def _commit(self, tok, reads, writes):
        for k in reads:
            self.readers.setdefault(k, []).append(tok)
        for k in writes:
            self.lastw[k] = tok
            self.readers[k] = []

    def op(self, e, outs, ins, fn):
        reads = [k for a in ins for k in self.key(a)]
        writes = [k for a in outs for k in self.key(a)]
        writes += [k for k in reads if k in self.psum_names and k not in writes]
        deps = self._deps(reads, writes)
        if e == "pe" and self.fence is not None:
            nd = []
            for (src, val) in deps:
                if src in ("act", "dve"):
                    if self.cnt[src] <= val:
                        inst = self.fence[src]()
                        self.cnt[src] += 1
                        inst.then_inc(self.sem[src], 1)
                    val = val + 1
                nd.append((src, val))
            deps = nd
        self._wait(e, deps)
        inst = fn()
        self.cnt[e] += 1
        inst.then_inc(self.sem[e], 1)
        self._commit((e, self.cnt[e]), reads, writes)
        self.nops += 1

    def dma(self, q, out, in_, extra_reads=(), track_out=True):
        reads = self.key(in_) + [k for a in extra_reads for k in self.key(a)]
        writes = self.key(out) if track_out else []
        s = self.drr
        self.drr = (self.drr + 1) % self.ndma
        deps = self._deps(reads, writes)
        if self.dcnt[s] > 0:
            deps.append((s, self.dcnt[s]))
        self._wait(q, deps)
        self.dcnt[s] += 16
        self.eng[q].dma_start(out=out, in_=in_).then_inc(self.dsem[s], 16)
        self._commit((s, self.dcnt[s]), reads, writes)
        self.nops += 1

    def collective(self, out, in_, groups):
        reads = self.key(in_)
        writes = self.key(out)
        s = self.drr
        self.drr = (self.drr + 1) % self.ndma
        deps = self._deps(reads, writes)
        if self.dcnt[s] > 0:
            deps.append((s, self.dcnt[s]))
        self._wait("pool", deps)
        self.dcnt[s] += 1
        self.nc.gpsimd.collective_compute("AllReduce", ALU.add, replica_groups=groups,
                                          ins=[in_.opt()], outs=[out.opt()]).then_inc(self.dsem[s])
        self._commit((s, self.dcnt[s]), reads, writes)

    def finish(self):
        for s in range(self.ndma):
            if self.dcnt[s] > 0:
                self.nc.sync.wait_ge(self.dsem[s], self.dcnt[s])
        for e in ("pe", "act", "dve"):
            if self.cnt[e] > 0:
                self.nc.sync.wait_ge(self.sem[e], self.cnt[e])
        for sem in list(self.sem.values()) + list(self.dsem):
            self.nc.sync.sem_clear(sem)


def build_program(cfg=None):
    cfg = cfg or {}
    layers = cfg.get("layers", [0, 1])
    do_prompt = cfg.get("prompt", True)
    do_sample = cfg.get("sample", True)
    use_cc = cfg.get("cc", True)
    ptiles = cfg.get("ptiles", [0, 1])
    br = cfg.get("br", "ACB")
    nc = bass.Bass("TRN2", target_bir_lowering=False)
    S = Sched(nc)
    S.paranoid = cfg.get('paranoid', False)
    PE, ACT, DVE = nc.tensor, nc.scalar, nc.vector

    def din(name, shape):
        return nc.dram_tensor(name, list(shape), F32, kind="ExternalInput").ap()

    xp_d = din("xp", [1024, D])
    xs_d = din("xs", [1024, D])
    h0_d = din("h0", [NL, 2, 4096, 128])
    cond_d = din("cond", [128, 2 * KC])
    vec_d = din("vec", [NL, 128, NV])
    bc_d = din("bc", [NL, 128, 320])
    cst_d = din("cst", [128, NCST * 128])
    msk_d = din("msk", [128, 16])
    w_mod_d = din("w_mod", [NL, D, 3 * D])
    w_in_d = din("w_in", [NL, D, N_IN])
    w_pa_d = din("w_pa", [NL, 2048, D])
    w_pb_d = din("w_pb", [NL, 4096, D])
    w_pc_d = din("w_pc", [NL, 2048, D])
    w_o_d = din("w_o", [NL, D, D])
    yp_d = nc.dram_tensor("yp", [1024, D], F32, kind="ExternalOutput").ap()
    ys_d = nc.dram_tensor("ys", [1024, D], F32, kind="ExternalOutput").ap()
    st_d = nc.dram_tensor("st", [4, NL, 2, 4096, 128], F32, kind="ExternalOutput").ap()

    def dscr(name, shape):
        return nc.dram_tensor(name, list(shape), F32).ap()

    yscr = {(k, t): dscr("yscr_%s%d" % (k, t), [128, KC * T]) for k in ("p", "s") for t in range(2)}
    h0t = {(l, d): dscr("h0t_%d_%d" % (l, d), [128, 4096]) for l in range(NL) for d in range(2)}
    sa_f = dscr("sa_f", [128, 4096])
    sb_t = [dscr("sb_t%d" % t, [128, 4096]) for t in range(2)]
    hinf = dscr("hinf", [128, 4096])
    hinb = dscr("hinb", [128, 4096])
    hb0 = dscr("hb0", [128, 4096])
    fcar = dscr("fcar", [128, 4096])
    zer = dscr("zer", [128, 4096])
    ccin = [dscr("ccin%d" % k, [512, 2048]) for k in range(4)] + [dscr("ccin4", [512, 128])]
    ccout = [dscr("ccout%d" % k, [512, 2048]) for k in range(4)] + [dscr("ccout4", [512, 128])]

    def sb(name, shape, dt=F32):
        return nc.alloc_sbuf_tensor("s_" + name, list(shape), dt)

    class Arena:
        def __init__(self, name):
            self.name = name
            base0 = (nc.sbuf_base + 31) // 32 * 32
            self.t = nc.alloc_sbuf_tensor(name, [128, 8192], F32)
            self.base = base0
            self.n = 0

        def view(self, off, shape, dt):
            nbytes = shape[1] * (4 if dt == F32 else 2)
            self.n += 1
            t = nc.alloc_sbuf_tensor_at("%s_v%d" % (self.name, self.n), list(shape), dt, offset=self.base + off)
            p0, p1 = off // 4096, (off + nbytes - 1) // 4096
            S.alias[t.name] = ["%s_p%d" % (self.name, p) for p in range(p0, p1 + 1)]
            return t

    ar0 = Arena("big0")
    ar1 = Arena("big1")
    big0 = [ar0.view(i * 4096, [128, 1024], F32) for i in range(8)]
    big1 = [ar1.view(i * 4096, [128, 1024], F32) for i in range(8)]
    yaT = [ar0.view(j * 1024, [128, T], BF16) for j in range(KC)]
    ycT = [ar0.view(16384 + j * 1024, [128, T], BF16) for j in range(KC)]
    mT = [ar0.view(n * 2048, [128, T], F32) for n in range(KC)]
    Rt = [ar0.view(d * 4096, [128, 1024], BF16) for d in range(2)]
    LT = [ar0.view(8192 + d * 2048, [128, 1024], BF16) for d in range(2)]
    MT = [ar0.view(12288 + d * 2048, [128, 1024], BF16) for d in range(2)]
    hprev = [[ar0.view(16384 + (b * 2 + d) * 1024, [128, 512], BF16) for d in range(2)] for b in range(NB)]
    xdt = [[ar0.view(24576 + (b * 2 + d) * 1024, [128, 512], BF16) for d in range(2)] for b in range(NB)]
    xTt = [ar1.view(k * 2048, [128, T], F32) for k in range(KC)]
    ybT = [ar1.view(c * 1024, [128, T], BF16) for c in range(32)]
    ybTg = [ar1.view(g * 4096, [128, 4 * T], BF16) for g in range(8)]

    uT = [sb("uT%d" % k, [128, T], BF16) for k in range(KC)]
    mg = [sb("mg%d" % k, [128, T], BF16) for k in range(KC)]
    NRING = 8
    ring = [sb("ring%d" % i, [128, 2048], BF16) for i in range(NRING)]
    cstf = sb("cstf", [128, NCST * 128])
    cstb = sb("cstb", [128, NCST * 128], BF16)
    vecs = sb("vecs", [128, NV])
    bcts = sb("bcts", [128, 320])
    msk = sb("msk", [128, 16])
    cond = sb("cond", [128, 2 * KC])
    scond = sb("scond", [128, 2 * KC], BF16)
    modv = sb("modv", [128, 2, 48])
    gsv = sb("gsv", [128, 2, KC])
    ggv = sb("ggv", [128, 2, KC])
    negA = sb("negA", [128, 128])
    rstd_b = sb("rstd_b", [128, T])
    tmpA = [sb("tmpA%d" % i, [128, T]) for i in range(4)]
    tmpB = [sb("tmpB%d" % i, [128, T], BF16) for i in range(4)]
    dtt = [sb("dtt%d" % b, [128, 128]) for b in range(NB)]
    at = [sb("at%d" % b, [128, 128]) for b in range(NB)]
    ecum = [sb("ecum%d" % b, [128, 128]) for b in range(NB)]
    dtd = [sb("dtd%d" % b, [128, 128]) for b in range(NB)]
    cdb = [sb("cdb%d" % b, [128, 128]) for b in range(NB)]
    tot_acc = sb("tot_acc", [128, 128])
    cumt = sb("cumt", [128, 128])
    xcT = [sb("xcT%d" % j, [128, T], BF16) for j in range(4)]
    bmT = sb("bmT", [128, T], BF16)
    cmT = sb("cmT", [128, T], BF16)
    xg = [sb("xg%d" % b, [128, 512], BF16) for b in range(NB)]
    bmg = [sb("bmg%d" % b, [128, 128], BF16) for b in range(NB)]
    szg = [sb("szg%d" % b, [128, 512], BF16) for b in range(NB)]
    xdd = [sb("xdd%d" % i, [128, 512], BF16) for i in range(2)]
    xds = sb("xds", [128, 512], BF16)
    hT = [sb("hT%d" % d, [128, 512]) for d in range(2)]
    smk = [sb("smk%d" % d, [128, 128], BF16) for d in range(2)]
    ygt = sb("ygt", [128, 512], BF16)
    sttmp = sb("sttmp", [128, 512])
    ostage = [sb("ostage%d" % i, [128, 512]) for i in range(2)]
    xstg = [ar1.view(i * 2048, [128, 512], F32) for i in range(2)]
    logD = sb("logD", [128, 2, 128])
    lgq = sb("lgq", [128, 128])
    gath = sb("gath", [128, 4, 128])
    dmt = sb("dmt", [128, 128])

    pbk = [nc.alloc_psum_tensor("pb%d" % i, [128, 512], F32) for i in range(7)]
    ptb = nc.alloc_psum_tensor("ptb", [128, 1024], BF16)
    S.psum_names = set(t.name for t in pbk) | {ptb.name}
    rot = {"i": 0, "set": [0, 1, 2, 3, 6]}

    def bank():
        rs = rot["set"]
        b = pbk[rs[rot["i"] % len(rs)]]
        rot["i"] += 1
        return b

    B4, B5 = pbk[4], pbk[5]

    fz = [sb("fz%d" % i, [128, 8]) for i in range(2)]
    if cfg.get("fence", False):
        S.fence = {"dve": lambda: DVE.memset(fz[0][:, :], 0.0), "act": lambda: ACT.activation(out=fz[1][:, 0:4], in_=fz[1][:, 4:8], func=AF.Copy)}

    def cf(i):
        return cstf[:, i * 128:(i + 1) * 128]

    def cb(i):
        return cstb[:, i * 128:(i + 1) * 128]

    ID, TRIF, TRIB, UF, UB, ONES = range(6)

    def isap(a):
        return a is not None and not isinstance(a, (int, float))

    def V(out, in0, in1, op):
        S.op("dve", [out], [in0, in1], lambda: DVE.tensor_tensor(out=out, in0=in0, in1=in1, op=op))

    def VS(out, in0, s1, s2, op0, op1=None):
        rd = [in0] + [a for a in (s1, s2) if isap(a)]
        if op1 is None:
            S.op("dve", [out], rd, lambda: DVE.tensor_scalar(out=out, in0=in0, scalar1=s1, scalar2=None, op0=op0))
        else:
            S.op("dve", [out], rd, lambda: DVE.tensor_scalar(out=out, in0=in0, scalar1=s1, scalar2=s2, op0=op0, op1=op1))

    def STT(out, in0, sc, in1, op0, op1):
        rd = [in0, in1] + ([sc] if isap(sc) else [])
        S.op("dve", [out], rd, lambda: DVE.scalar_tensor_tensor(out=out, in0=in0, scalar=sc, in1=in1, op0=op0, op1=op1))

    def VC(out, in_):
        S.op("dve", [out], [in_], lambda: DVE.tensor_copy(out=out, in_=in_))

    def VR(out, in_):
        S.op("dve", [out], [in_], lambda: DVE.reciprocal(out=out, in_=in_))

    def A(out, in_, func, bias=None, scale=None):
        rd = [in_] + [a for a in (bias, scale) if isap(a)]
        kw = {}
        if bias is not None:
            kw["bias"] = bias
        if scale is not None:
            kw["scale"] = scale
        S.op("act", [out], rd, lambda: ACT.activation(out=out, in_=in_, func=func, **kw))

    def MMG(outs, ins, mms):
        def fn():
            inst = None
            for (o, l, r, st, sp) in mms:
                inst = PE.matmul(o, lhsT=l, rhs=r, start=st, stop=sp)
            return inst
        S.op("pe", outs, ins, fn)

    def TRB(out_t, pairs):
        def fn():
            inst = None
            for (o, i) in pairs:
                inst = PE.transpose(o, i, cb(ID))
            return inst
        S.op("pe", [out_t], [p[1] for p in pairs] + [cstb], fn)

    def TRF(out_t, pairs):
        MMG([out_t], [p[1] for p in pairs] + [cstf], [(o, i, cf(ID), True, True) for (o, i) in pairs])

    def rsqrt_from(out, in_, scale, bias):
        A(out, in_, AF.Sqrt, bias=bias, scale=scale)
        VR(out, out)

    ring_i = {"i": 0}

    def wload(src):
        views = []
        for r0 in range(0, src.shape[0], 2048):
            slot = ring[ring_i["i"] % NRING]
            ring_i["i"] += 1
            view = slot[:, :].rearrange("p (k n) -> p k n", k=16)
            S.dma("pool", view, src[r0:r0 + 2048, :].rearrange("(k p) n -> p k n", p=128))
            views.append(view)
        return views

    def proj_fm(wvs, out_bank, src=None):
        src = src or uT
        kcs = 16 * len(wvs)
        mm = [(out_bank[:, :], wvs[k // 16][:, k % 16, :], src[k][:, :], k == 0, k == kcs - 1) for k in range(kcs)]
        MMG([out_bank], list(wvs) + [src[k] for k in range(kcs)], mm)

    for sem in list(S.sem.values()) + list(S.dsem):
        nc.sync.sem_clear(sem)
    nc.all_engine_barrier()
    S.dma("sp", cstf[:, :], cst_d)
    S.dma("pool", cstb[:, :], cst_d)
    S.dma("sp", msk[:, :], msk_d)
    S.dma("sp", cond[:, :], cond_d)
    for i in range(4):
        S.op("dve", [big0[i]], [], lambda i=i: DVE.memset(big0[i][:, :], 0.0))
    for i in range(4):
        S.dma("sp", zer[:, i * 1024:(i + 1) * 1024], big0[i][:, :])
    A(tmpA[0][:, 0:32], cond[:, :], AF.Silu)
    VC(scond[:, :], tmpA[0][:, 0:32])
    if do_sample:
        for l in layers:
            for d in range(2):
                for c8 in range(4):
                    stg = big0[4 + c8 % 2]
                    S.dma("sp", stg[:, :].rearrange("p (c n) -> p c n", c=8),
                          h0_d[l, d, c8 * 1024:(c8 + 1) * 1024, :].rearrange("(c p) n -> p c n", p=128))
                    for q4 in range(2):
                        bk = bank()
                        TRF(bk, [(bk[:, j * 128:(j + 1) * 128], stg[:, (q4 * 4 + j) * 128:(q4 * 4 + j + 1) * 128]) for j in range(4)])
                        tt = tmpA[q4]
                        A(tt[:, :], bk[:, :], AF.Copy)
                        col = (c8 * 8 + q4 * 4) * 128
                        S.dma("sp", h0t[(l, d)][:, col:col + 512], tt[:, :])

    def mod_phase(l):
        S.dma("sp", vecs[:, :], vec_d[l])
        S.dma("sp", bcts[:, :], bc_d[l])
        for n in range(48):
            wv = wload(w_mod_d[l][:, n * 128:(n + 1) * 128])[0]
            bk = bank()
            mm = [(bk[:, 0:2], wv[:, k, :], scond[:, k:k + KC + 1:KC], k == 0, k == KC - 1) for k in range(KC)]
            MMG([bk], [wv, scond], mm)
            for c in range(2):
                VS(modv[:, c, n:n + 1], bk[:, c:c + 1], vecs[:, V_BMOD + n:V_BMOD + n + 1], None, ALU.add)
        for c in range(2):
            STT(gsv[:, c, :], modv[:, c, 16:32], 1.0, vecs[:, V_GPRE:V_GPRE + 16], ALU.add, ALU.mult)
            V(ggv[:, c, :], modv[:, c, 32:48], vecs[:, V_GPOST:V_GPOST + 16], ALU.mult)
        A(negA[:, :], bcts[:, 128:256], AF.Exp)
        VS(negA[:, :], negA[:, :], -1.0, None, ALU.mult)

    def load_x_tokmajor(src_rows):
        for b in range(NB):
            for hh in range(2):
                S.dma("sp", big0[b * 2 + hh][:, :], src_rows[b * 128:(b + 1) * 128, hh * 1024:(hh + 1) * 1024])
        for k in range(KC):
            bk = bank()
            TRF(bk, [(bk[:, b * 128:(b + 1) * 128], big0[b * 2 + k // 8][:, (k % 8) * 128:(k % 8 + 1) * 128]) for b in range(NB)])
            A(xTt[k][:, :], bk[:, :], AF.Copy)

    def load_x_featmajor(scr):
        for i in range(8):
            S.dma("sp", big1[i][:, :], scr[:, i * 1024:(i + 1) * 1024])

    def rms_bcast(srcs, denom, out_rstd):
        n = len(srcs)
        for k in range(n):
            sq = tmpB[k % 4]
            A(sq[:, :], srcs[k][:, :], AF.Square)
            MMG([B4], [cstb, sq], [(B4[:, :], cb(ONES), sq[:, :], k == 0, k == n - 1)])
        rsqrt_from(out_rstd[:, :], B4[:, :], 1.0 / denom, EPS)

    def make_u(c):
        rms_bcast(xTt, float(D), rstd_b)
        for k in range(KC):
            t = tmpA[k % 4]
            V(t[:, :], xTt[k][:, :], rstd_b[:, :], ALU.mult)
            A(uT[k][:, :], t[:, :], AF.Identity, bias=modv[:, c, k:k + 1], scale=gsv[:, c, k:k + 1])

    cacc = None
    dgw = sb("dgw", [128, 31 * 128], BF16)
    gpad = sb("gpad", [128, 752], BF16)

    def conv_fm(out, src, wcol, K, seg, bias=None):
        c = K // 2
        if K > 8:
            return conv_fm2(out, src, wcol, K, seg, bias)
        if bias is None:
            VS(out, src, wcol(c), None, ALU.mult)
        else:
            VS(out, src, wcol(c), bias, ALU.mult, ALU.add)
        ov = out.rearrange("p (s t) -> p s t", t=seg)
        sv = src.rearrange("p (s t) -> p s t", t=seg)
        for k in range(K):
            o = k - c
            if o == 0 or abs(o) >= seg:
                continue
            if o > 0:
                STT(ov[:, :, 0:seg - o], sv[:, :, o:seg], wcol(k), ov[:, :, 0:seg - o], ALU.mult, ALU.add)
            else:
                STT(ov[:, :, -o:seg], sv[:, :, 0:seg + o], wcol(k), ov[:, :, -o:seg], ALU.mult, ALU.add)

    def conv_fm2(out, src, wcol, K, seg, bias):
        c = K // 2
        VS(out, src, wcol(c), bias, ALU.mult, ALU.add)
        acc2 = cacc[:, :]
        S.op("dve", [cacc], [], lambda: DVE.memset(cacc[:, :], 0.0))
        accs = [out, acc2]
        sv = src.rearrange("p (s t) -> p s t", t=seg)
        i = 0
        for k in range(K):
            o = k - c
            if o == 0 or abs(o) >= seg:
                continue
            i += 1
            ov = accs[i % 2].rearrange("p (s t) -> p s t", t=seg)
            if o > 0:
                STT(ov[:, :, 0:seg - o], sv[:, :, o:seg], wcol(k), ov[:, :, 0:seg - o], ALU.mult, ALU.add)
            else:
                STT(ov[:, :, -o:seg], sv[:, :, 0:seg + o], wcol(k), ov[:, :, -o:seg], ALU.mult, ALU.add)
        V(out, out, acc2, ALU.add)

    def out_proj(l, wd, src, kcs, gate_c0, first, post=None):
        for n in range(KC):
            if True:
                wg = wload(w_in_d[l][:, gate_c0 + n * 128:gate_c0 + (n + 1) * 128])
                wv = wload(wd[l][:, n * 128:(n + 1) * 128])
                bg = bank()
                proj_fm(wg, bg)
                gt = tmpA[n % 2]
                A(gt[:, :], bg[:, :], AF.Sigmoid)
                bk = bank()
                proj_fm(wv, bk, src=src)
                if post is not None:
                    V(gt[:, :], gt[:, :], post[:, :], ALU.mult)
                if first:
                    V(mg[n][:, :], bk[:, :], gt[:, :], ALU.mult)
                else:
                    t2 = tmpA[2 + n % 2]
                    V(t2[:, :], bk[:, :], gt[:, :], ALU.mult)
                    V(mg[n][:, :], mg[n][:, :], t2[:, :], ALU.add)

    def vcol(off):
        return vecs[:, off:off + 1]

    def branch_A(l, seg):
        for j in range(KC):
            if True:
                h2 = j % 2
                wb = wload(w_in_d[l][:, A_B + j * 128:A_B + (j + 1) * 128])
                wc = wload(w_in_d[l][:, A_C + j * 128:A_C + (j + 1) * 128])
                wh = wload(w_in_d[l][:, A_H + j * 128:A_H + (j + 1) * 128])
                wz = wload(w_in_d[l][:, A_Z + j * 128:A_Z + (j + 1) * 128])
                pb_, pc_ = bank(), bank()
                proj_fm(wb, pb_)
                proj_fm(wc, pc_)
                tb, tc = tmpA[h2 * 2], tmpA[h2 * 2 + 1]
                A(tb[:, :], pb_[:, :], AF.Copy)
                A(tc[:, :], pc_[:, :], AF.Copy)
                ph_, pz_ = bank(), bank()
                proj_fm(wh, ph_)
                proj_fm(wz, pz_)
                V(tc[:, :], ph_[:, :], tc[:, :], ALU.mult)
                cv = sttmp
                conv_fm(cv[:, :], tc[:, :], lambda k, j=j: vcol(V_CAW + j * 3 + k), 3, seg)
                V(cv[:, :], cv[:, :], tb[:, :], ALU.mult)
                sz = tmpB[h2]
                A(sz[:, :], pz_[:, :], AF.Silu)
                V(yaT[j][:, :], cv[:, :], sz[:, :], ALU.mult)
        out_proj(l, w_pa_d, yaT, KC, G_A, True)

    def branch_C(l, seg):
        S.op("dve", [gpad], [], lambda: DVE.memset(gpad[:, :], 0.0))
        for j in range(KC):
            if True:
                h2 = j % 2
                wa = wload(w_in_d[l][:, C_A + j * 128:C_A + (j + 1) * 128])
                wg = wload(w_in_d[l][:, C_G + j * 128:C_G + (j + 1) * 128])
                pa_, pg_ = bank(), bank()
                proj_fm(wa, pa_)
                proj_fm(wg, pg_)
                sg = tmpA[h2]
                A(sg[:, :], pg_[:, :], AF.Sigmoid)
                gw = seg + 30
                nseg = T // seg
                nh = nseg // 2
                Wh = nh * gw
                Nh = Wh - 30
                gv = gpad[:, 0:nseg * gw].rearrange("p (s w) -> p s w", w=gw)
                V(gv[:, :, 15:15 + seg], pa_[:, :].rearrange("p (s t) -> p s t", t=seg),
                  sg[:, :].rearrange("p (s t) -> p s t", t=seg), ALU.mult)
                V(dgw[:, :].rearrange("p (k n) -> p k n", k=31), cb(ID).unsqueeze(1).to_broadcast([128, 31, 128]),
                  vecs[:, V_CCW + j * 31:V_CCW + (j + 1) * 31].unsqueeze(2).to_broadcast([128, 31, 128]), ALU.mult)
                sq = tmpB[2 + h2]
                for hf in range(2):
                    cvb = bank()
                    mm = [(cvb[:, 0:Nh], dgw[:, k * 128:(k + 1) * 128], gpad[:, hf * Wh + k:hf * Wh + k + Nh], k == 0, k == 30)
                          for k in range(31)]
                    MMG([cvb], [dgw, gpad], mm)
                    cvv = cvb[:, 0:Wh].rearrange("p (s w) -> p s w", w=gw)[:, :, 0:seg]
                    yv_ = ycT[j][:, hf * 256:(hf + 1) * 256].rearrange("p (s t) -> p s t", t=seg)
                    sv_ = sq[:, hf * 256:(hf + 1) * 256].rearrange("p (s t) -> p s t", t=seg)
                    A(yv_, cvv, AF.Identity, bias=vcol(V_CCB + j))
                    A(sv_, cvv, AF.Square, bias=vcol(V_CCB + j))
                MMG([B4], [cstb, ycT[j]], [(B4[:, :], cb(ONES), ycT[j][:, :], j == 0, j == KC - 1)])
                MMG([B5], [cstb, sq], [(B5[:, :], cb(ONES), sq[:, :], j == 0, j == KC - 1)])
        mean, rs = tmpA[2], tmpA[3]
        VS(mean[:, :], B4[:, :], 1.0 / 2048, None, ALU.mult)
        V(rs[:, :], mean[:, :], mean[:, :], ALU.mult)
        STT(rs[:, :], B5[:, :], 1.0 / 2048, rs[:, :], ALU.mult, ALU.subtract)
        rsqrt_from(rs[:, :], rs[:, :], 1.0, EPS)
        V(mean[:, :], mean[:, :], rs[:, :], ALU.mult)
        for j in range(KC):
            if True:
                h2 = j % 2
                wz = wload(w_in_d[l][:, C_Z + j * 128:C_Z + (j + 1) * 128])
                pz_ = bank()
                proj_fm(wz, pz_)
                t = tmpA[h2]
                V(t[:, :], ycT[j][:, :], rs[:, :], ALU.mult)
                V(t[:, :], t[:, :], mean[:, :], ALU.subtract)
                A(t[:, :], t[:, :], AF.Silu, bias=vcol(V_LNB + j), scale=vcol(V_LNG + j))
                sz = tmpB[h2]
                A(sz[:, :], pz_[:, :], AF.Silu)
                V(ycT[j][:, :], t[:, :], sz[:, :], ALU.mult)
        out_proj(l, w_pc_d, ycT, KC, G_C, br[0] == 'C')

    def ssd_dt(l, full):
        wv = wload(w_in_d[l][:, B_DT:B_DT + 128])[0]
        for b in range(NB):
            bk = bank()
            mm = [(bk[:, 0:128], uT[k][:, b * 128:(b + 1) * 128], wv[:, k, :], k == 0, k == KC - 1) for k in range(KC)]
            MMG([bk], [wv] + uT, mm)
            V(dtt[b][:, :], bk[:, 0:128], bcts[:, 0:128], ALU.add)
            A(dtt[b][:, :], dtt[b][:, :], AF.Exp)
            A(dtt[b][:, :], dtt[b][:, :], AF.Ln, bias=1.0)
            V(at[b][:, :], dtt[b][:, :], negA[:, :], ALU.mult)
            bk2 = bank()
            MMG([bk2], [cstf, at[b]], [
                (bk2[:, 0:64], cf(TRIF), at[b][:, 0:64], True, True),
                (bk2[:, 64:128], cf(TRIB), at[b][:, 64:128], True, True),
                (bk2[:, 128:256], cf(ONES), at[b][:, :], True, True)])
            A(cumt[:, :], bk2[:, 0:128], AF.Copy)
            if full:
                A(ecum[b][:, :], bk2[:, 0:128], AF.Exp)
            V(dtd[b][:, :], bk2[:, 128:256], cumt[:, :], ALU.subtract)
            A(dtd[b][:, :], dtd[b][:, :], AF.Exp)
            V(dtd[b][:, :], dtd[b][:, :], dtt[b][:, :], ALU.mult)
            A(cdb[b][:, :], bk2[:, 128:256], AF.Exp)
            if b == 0:
                A(tot_acc[:, :], bk2[:, 128:256], AF.Copy)
            else:
                V(tot_acc[:, :], tot_acc[:, :], bk2[:, 128:256], ALU.add)

    def hsl(d, g):
        return slice(d * 64 + g * 8, d * 64 + g * 8 + 8)

    def bc8(ap):
        return ap.unsqueeze(2).to_broadcast([128, 8, 64])

    def v3(ap):
        return ap.rearrange("p (h q) -> p h q", h=8)

    def ssd_group(l, g, full, seg, kind, init_src, end_dst, st_out):
        for idx in range(6):
            if idx == 4:
                it = (wload(w_in_d[l][:, B_BM + g * 128:B_BM + (g + 1) * 128]), 32 + g, bmT)
            elif idx == 5:
                if not full:
                    continue
                it = (wload(w_in_d[l][:, B_CM + g * 128:B_CM + (g + 1) * 128]), 40 + g, cmT)
            else:
                it = (wload(w_in_d[l][:, B_X + g * 512 + idx * 128:B_X + g * 512 + (idx + 1) * 128]), g * 4 + idx, xcT[idx])
            (wv, ci, dst) = it
            bk = bank()
            proj_fm(wv, bk)
            cv = sttmp
            conv_fm(cv[:, :], bk[:, :], lambda k, ci=ci: vcol(V_SCW + ci * 5 + k), 5, seg, bias=vcol(V_SCB + ci))
            A(dst[:, :], cv[:, :], AF.Silu)
        for b in range(NB):
            prs = [(ptb[:, j * 128:(j + 1) * 128], xcT[j][:, b * 128:(b + 1) * 128]) for j in range(4)]
            prs.append((ptb[:, 512:640], bmT[:, b * 128:(b + 1) * 128]))
            TRB(ptb, prs)
            A(xg[b][:, :], ptb[:, 0:512], AF.Copy)
            VC(bmg[b][:, :], ptb[:, 512:640])
        if full:
            wzs = [wload(w_in_d[l][:, B_Z + g * 512 + i * 128:B_Z + g * 512 + (i + 1) * 128])[0] for i in range(4)]
            for b in range(NB):
                bk = bank()
                mm = [(bk[:, i * 128:(i + 1) * 128], uT[k][:, b * 128:(b + 1) * 128], wzs[i][:, k, :], k == 0, k == KC - 1)
                      for i in range(4) for k in range(KC)]
                MMG([bk], wzs + uT, mm)
                A(szg[b][:, :], bk[:, :], AF.Silu)
        for b in range(NB if full else 0):
            for d in range(2):
                V(v3(xdt[b][d][:, :]), v3(xg[b][:, :]), bc8(dtt[b][:, hsl(d, g)]), ALU.mult)
        orders = [list(range(NB)), list(range(NB - 1, -1, -1))]
        if kind == "s":
            for d in range(2):
                S.dma("sp", hT[d][:, :], init_src[d][:, g * 512:(g + 1) * 512])
        for pos in range(NB):
            for d in range(2):
                b = orders[d][pos]
                seq_first = (kind == "p" and pos % 2 == 0)
                seq_last = (kind == "p" and pos % 2 == 1)
                xd = xdd[d]
                V(v3(xd[:, :]), v3(xg[b][:, :]), bc8(dtd[b][:, hsl(d, g)]), ALU.mult)
                if full and not seq_first:
                    A(hprev[b][d][:, :], hT[d][:, :], AF.Copy)
                bk = bank()
                MMG([bk], [bmg[b], xd], [(bk[:, :], bmg[b][:, :], xd[:, :], True, True)])
                if seq_first:
                    A(hT[d][:, :], bk[:, :], AF.Copy)
                else:
                    V(v3(hT[d][:, :]), v3(hT[d][:, :]), bc8(cdb[b][:, hsl(d, g)]), ALU.mult)
                    V(hT[d][:, :], hT[d][:, :], bk[:, :], ALU.add)
                if seq_last:
                    seq = b // 2
                    bk2 = bank()
                    TRF(bk2, [(bk2[:, j * 128:(j + 1) * 128], hT[d][:, j * 128:(j + 1) * 128]) for j in range(4)])
                    so = ostage[d]
                    A(so[:, :], bk2[:, :], AF.Copy)
                    S.dma("sp", st_out(seq, d)[g * 512:(g + 1) * 512, :].rearrange("(j p) n -> p j n", p=128),
                          so[:, :].rearrange("p (j n) -> p j n", j=4), track_out=False)
        for d in range(2):
            if kind == "s" and end_dst is not None and end_dst[d] is not None:
                S.dma("sp", end_dst[d][:, g * 512:(g + 1) * 512], hT[d][:, :])
        if not full:
            return
        YB = [B5, pbk[6]]

        def y_front(b):
            tok = slice(b * 128, (b + 1) * 128)
            Y = YB[b % 2]
            V(v3(xds[:, :]), v3(xg[b][:, :]), bc8(bcts[:, 256 + g * 8:256 + g * 8 + 8]), ALU.mult)
            MMG([B4], [bmT, cmT], [(B4[:, 0:128], bmT[:, tok], cmT[:, tok], True, True)])
            V(smk[0][:, :], B4[:, 0:128], cf(TRIF), ALU.mult)
            V(smk[1][:, :], B4[:, 0:128], cf(TRIB), ALU.mult)
            MMG([Y], [cstb, xds], [(Y[:, :], cb(ID), xds[:, :], True, False)])
            for d in range(2):
                tri = cb(TRIF) if d == 0 else cb(TRIB)
                V(Rt[d][:, :].rearrange("p (h i) -> p h i", h=8),
                  at[b][:, hsl(d, g)].unsqueeze(2).to_broadcast([128, 8, 128]),
                  tri.unsqueeze(1).to_broadcast([128, 8, 128]), ALU.mult)
                U = cb(UF) if d == 0 else cb(UB)
                for hf in range(2):
                    bk = bank()
                    MMG([bk], [cstb, Rt[d]], [(bk[:, :], U, Rt[d][:, hf * 512:(hf + 1) * 512], True, True)])
                    A(LT[d][:, hf * 512:(hf + 1) * 512], bk[:, :], AF.Exp)
                V(MT[d][:, :].rearrange("p (h i) -> p h i", h=8), LT[d][:, :].rearrange("p (h i) -> p h i", h=8),
                  smk[d][:, :].unsqueeze(1).to_broadcast([128, 8, 128]), ALU.mult)
                mm = [(Y[:, h * 64:(h + 1) * 64], MT[d][:, h * 128:(h + 1) * 128], xdt[b][d][:, h * 64:(h + 1) * 64],
                       False, (d == 1 and h == 7)) for h in range(8)]
                MMG([Y], [MT[d], xdt[b][d]], mm)

        def y_back(b):
            tok = slice(b * 128, (b + 1) * 128)
            Y = YB[b % 2]
            yo = []
            for d in range(2):
                first_blk = (kind == "p" and ((d == 0 and b % 2 == 0) or (d == 1 and b % 2 == 1)))
                if not first_blk:
                    bk = bank()
                    MMG([bk], [cmT, hprev[b][d]], [(bk[:, :], cmT[:, tok], hprev[b][d][:, :], True, True)])
                    yo.append((bk, d))
            yv = tmpA[b % 2]
            prev = Y
            for (bk, d) in yo:
                t = tmpA[2 + d]
                V(v3(t[:, :]), v3(bk[:, :]), bc8(ecum[b][:, hsl(d, g)]), ALU.mult)
                V(yv[:, :], prev[:, :], t[:, :], ALU.add)
                prev = yv
            V(ygt[:, :], prev[:, :], szg[b][:, :], ALU.mult)
            TRB(ptb, [(ptb[:, j * 128:(j + 1) * 128], ygt[:, j * 128:(j + 1) * 128]) for j in range(4)])
            A(ybTg[g][:, :].rearrange("p (j t) -> p j t", j=4)[:, :, tok],
              ptb[:, 0:512].rearrange("p (j t) -> p j t", j=4), AF.Copy)

        rot["set"] = [0, 1, 2, 3]
        y_front(0)
        for b in range(NB):
            if b + 1 < NB:
                y_front(b + 1)
            y_back(b)
        rot["set"] = [0, 1, 2, 3, 6]

    def branch_B_tail(l):
        rsb = rstd_b
        rms_bcast(ybT, 4096.0, rsb)
        for c in range(32):
            VS(ybT[c][:, :], ybT[c][:, :], vcol(V_SNG + c), None, ALU.mult)
        out_proj(l, w_pb_d, ybT, 32, G_B, br[0] == 'B', post=rsb)

    def final(l, c, first_layer, src_rows, scr_in, write_out):
        for n in range(KC):
            if True:
                wv = wload(w_o_d[l][:, n * 128:(n + 1) * 128])
                bk = bank()
                proj_fm(wv, bk, src=mg)
                A(mT[n][:, :], bk[:, :], AF.Copy)
                sq = tmpB[n % 4]
                A(sq[:, :], bk[:, :], AF.Square)
                MMG([B4], [cstb, sq], [(B4[:, :], cb(ONES), sq[:, :], n == 0, n == KC - 1)])
        rsqrt_from(rstd_b[:, :], B4[:, :], 1.0 / D, EPS)
        if not first_layer:
            load_x_featmajor(scr_in)
        for n in range(KC):
            if first_layer:
                xs_ = xstg[n % 2]
                S.dma("sp", xs_[:, :].rearrange("p (b f) -> p b f", b=NB),
                      src_rows[:, n * 128:(n + 1) * 128].rearrange("(b p) f -> p b f", p=128))
                xb_ = bank()
                TRF(xb_, [(xb_[:, b * 128:(b + 1) * 128], xs_[:, b * 128:(b + 1) * 128]) for b in range(NB)])
                xin = xb_[:, :]
            else:
                xin = xTt[n][:, :]
            t = tmpA[n % 4]
            V(t[:, :], mT[n][:, :], rstd_b[:, :], ALU.mult)
            STT(t[:, :], t[:, :], ggv[:, c, n:n + 1], xin, ALU.mult, ALU.add)
            write_out(n, t)

    def writer_scratch(scr):
        def w(n, t):
            S.dma("sp", scr[:, n * T:(n + 1) * T], t[:, :])
        return w

    def writer_tokmajor(dst_rows):
        def w(n, t):
            bk = bank()
            TRF(bk, [(bk[:, b * 128:(b + 1) * 128], t[:, b * 128:(b + 1) * 128]) for b in range(NB)])
            so = ostage[n % 2]
            A(so[:, :], bk[:, :], AF.Copy)
            S.dma("sp", dst_rows[:, n * 128:(n + 1) * 128].rearrange("(b p) f -> p b f", p=128),
                  so[:, :].rearrange("p (b f) -> p b f", b=NB), track_out=False)
        return w

    def tile_pass(l, kind, t, full, init_src=None, end_dst=None):
        c = 0 if kind == "p" else 1
        seg = 256 if kind == "p" else 64
        src_rows = (xp_d if kind == "p" else xs_d)[t * T:(t + 1) * T, :]
        first_layer = (l == layers[0])
        last_layer = (l == layers[-1])
        if first_layer:
            load_x_tokmajor(src_rows)
        else:
            load_x_featmajor(yscr[(kind, t)])
        make_u(c)
        if full and "A" in br:
            branch_A(l, seg)
        if full and "C" in br:
            branch_C(l, seg)
        ssd_dt(l, full)
        for g in range(G):
            ssd_group(l, g, full, seg, kind, init_src, end_dst,
                      (lambda seq, d: st_d[t * 2 + seq, l, d]) if kind == "p" else None)
        if not full:
            return
        if "B" in br:
            branch_B_tail(l)
        if last_layer:
            dst = (yp_d if kind == "p" else ys_d)[t * T:(t + 1) * T, :]
            final(l, c, first_layer, src_rows, yscr[(kind, t)], writer_tokmajor(dst))
        else:
            final(l, c, first_layer, src_rows, yscr[(kind, t)], writer_scratch(yscr[(kind, t)]))

    GROUPS = [[0, 1, 2, 3], [4, 5, 6, 7]]

    def ld4(pieces, src):
        for i in range(4):
            S.dma("sp", pieces[i][:, :], src[:, i * 1024:(i + 1) * 1024])

    def st4(dst, pieces):
        for i in range(4):
            S.dma("sp", dst[:, i * 1024:(i + 1) * 1024], pieces[i][:, :])

    def bc64(ap):
        return ap.unsqueeze(2).to_broadcast([128, 16, 64])

    def p3(piece):
        return piece[:, :].rearrange("p (h q) -> p h q", h=16)

    def exchange(l):
        Sb, Tm = big0[0:4], big0[4:8]
        ld4(Sb, sb_t[1])
        ld4(Tm, sb_t[0])
        A(dmt[:, :], logD[:, 0, :], AF.Exp)
        for i in range(4):
            V(p3(Sb[i]), p3(Sb[i]), bc64(dmt[:, 64 + 16 * i:64 + 16 * i + 16]), ALU.mult)
            V(Sb[i][:, :], Sb[i][:, :], Tm[i][:, :], ALU.add)
        Sf = big1[0:4]
        ld4(Sf, sa_f)
        V(lgq[:, :], logD[:, 0, :], logD[:, 1, :], ALU.add)
        Tm2 = big1[4:8]
        for r in range(4):
            for (k, src) in ((0, Sf[0:2]), (1, Sf[2:4]), (2, Sb[0:2]), (3, Sb[2:4])):
                for i in range(2):
                    tt = Tm2[(k * 2 + i) % 4]
                    VS(tt[:, :], src[i][:, :], msk[:, r:r + 1], None, ALU.mult)
                    S.dma("sp", ccin[k][r * 128:(r + 1) * 128, i * 1024:(i + 1) * 1024], tt[:, :])
            VS(tmpA[r][:, 0:128], lgq[:, :], msk[:, r:r + 1], None, ALU.mult)
            S.dma("sp", ccin[4][r * 128:(r + 1) * 128, :], tmpA[r][:, 0:128])
        for k in range(5):
            if use_cc:
                S.collective(ccout[k], ccin[k], GROUPS)
            else:
                for r in range(4):
                    S.dma("sp", ccout[k][r * 128:(r + 1) * 128, :], ccin[k][r * 128:(r + 1) * 128, :])

    def fold(l):
        for r in range(4):
            S.dma("sp", gath[:, r, :], ccout[4][r * 128:(r + 1) * 128, :])
        run, Sr = big0[0:4], big0[4:8]
        for d in range(2):
            ld4(run, h0t[(l, d)])
            steps = [0, 1, 2] if d == 0 else [3, 2, 1]
            for si, r in enumerate(steps):
                mcol = msk[:, 4 + d * 3 + si:4 + d * 3 + si + 1]
                A(dmt[:, :], gath[:, r, :], AF.Exp, scale=mcol)
                for i in range(4):
                    S.dma("sp", Sr[i][:, :], ccout[d * 2 + i // 2][r * 128:(r + 1) * 128, (i % 2) * 1024:(i % 2 + 1) * 1024])
                for i in range(4):
                    V(p3(run[i]), p3(run[i]), bc64(dmt[:, d * 64 + 16 * i:d * 64 + 16 * i + 16]), ALU.mult)
                    STT(run[i][:, :], Sr[i][:, :], mcol, run[i][:, :], ALU.mult, ALU.add)
            st4(hinf if d == 0 else hinb, run)
        A(dmt[:, :], logD[:, 1, :], AF.Exp)
        ld4(Sr, sb_t[1])
        for i in range(4):
            V(p3(run[i]), p3(run[i]), bc64(dmt[:, 64 + 16 * i:64 + 16 * i + 16]), ALU.mult)
            V(run[i][:, :], run[i][:, :], Sr[i][:, :], ALU.add)
        st4(hb0, run)

    for l in layers:
        mod_phase(l)
        if do_sample:
            for t in range(2):
                tile_pass(l, "s", t, False, init_src=[zer if t == 0 else sa_f, zer], end_dst=[sa_f, sb_t[t]])
                VC(logD[:, t, :], tot_acc[:, :])
            exchange(l)
        if do_prompt:
            for t in ptiles:
                tile_pass(l, "p", t, True)
        if do_sample:
            fold(l)
            tile_pass(l, "s", 0, True, init_src=[hinf, hb0], end_dst=[fcar, None])
            tile_pass(l, "s", 1, True, init_src=[fcar, hinb], end_dst=[None, None])
    S.finish()
    return nc, S


def _fm(v):
    return np.ascontiguousarray(v.reshape(-1, 128).T)


def host_inputs(inp):
    f = lambda a: np.ascontiguousarray(np.asarray(a, dtype=np.float32))
    x_prompt, x_sample, state = f(inp["x_prompt"]), f(inp["x_sample"]), f(inp["state_ssd"])
    c, c_ctx = f(inp["c"]), f(inp["c_ctx"])
    vec = np.zeros((NL, 128, NV), np.float32)
    bc = np.zeros((NL, 128, 320), np.float32)
    for l in range(NL):
        vec[l, :, V_GPRE:V_GPRE + 16] = _fm(f(inp["g_pre"])[l])
        vec[l, :, V_GPOST:V_GPOST + 16] = _fm(f(inp["g_post"])[l])
        vec[l, :, V_CCB:V_CCB + 16] = _fm(f(inp["conf_conv_b"])[l])
        vec[l, :, V_LNG:V_LNG + 16] = _fm(f(inp["conf_ln_g"])[l])
        vec[l, :, V_LNB:V_LNB + 16] = _fm(f(inp["conf_ln_b"])[l])
        vec[l, :, V_SNG:V_SNG + 32] = _fm(f(inp["ssd_norm_g"])[l])
        vec[l, :, V_SCB:V_SCB + 48] = _fm(f(inp["ssd_conv_b"])[l])
        vec[l, :, V_BMOD:V_BMOD + 48] = _fm(f(inp["b_mod"])[l])
        caw = f(inp["conv_a_w"])[l]
        vec[l, :, V_CAW:V_CAW + 48] = caw.T.reshape(16, 128, 3).transpose(1, 0, 2).reshape(128, 48)
        scw = f(inp["ssd_conv_w"])[l]
        vec[l, :, V_SCW:V_SCW + 240] = scw.T.reshape(48, 128, 5).transpose(1, 0, 2).reshape(128, 240)
        ccw = f(inp["conf_conv_w"])[l]
        vec[l, :, V_CCW:V_CCW + 496] = ccw.T.reshape(16, 128, 31).transpose(1, 0, 2).reshape(128, 496)
        bc[l, :, 0:128] = f(inp["dt_bias"])[l].reshape(1, 128)
        bc[l, :, 128:256] = f(inp["a_log"])[l].reshape(1, 128)
        bc[l, :, 256:320] = f(inp["d_skip"])[l].reshape(1, 64)
    i = np.arange(128)
    m, j = i[:, None], i[None, :]
    cst = np.concatenate([(m == j), (m <= j), (m >= j), (m > j), (m < j), np.ones((128, 128), bool)], axis=1).astype(np.float32)
    shared = dict(vec=vec, bc=bc, cst=cst, w_mod=f(inp["w_mod"]), w_in=f(inp["w_in"]), w_pa=f(inp["w_pa"]),
                  w_pb=f(inp["w_pb"]), w_pc=f(inp["w_pc"]), w_o=f(inp["w_o"]))
    maps = []
    for core in range(8):
        b, q = core // 4, core % 4
        msk = np.zeros((128, 16), np.float32)
        msk[:, q] = 1.0
        for si, r in enumerate([0, 1, 2]):
            msk[:, 4 + si] = 1.0 if r < q else 0.0
        for si, r in enumerate([3, 2, 1]):
            msk[:, 7 + si] = 1.0 if r > q else 0.0
        cond = np.concatenate([_fm(c_ctx), _fm(c[b])], axis=1)
        mp = dict(shared)
        mp.update(xp=np.ascontiguousarray(x_prompt[4 * core:4 * core + 4].reshape(1024, D)),
                  xs=np.ascontiguousarray(x_sample[b, q * 1024:(q + 1) * 1024]),
                  h0=np.ascontiguousarray(state[b].reshape(NL, 2, 4096, 128)),
                  cond=np.ascontiguousarray(cond), msk=msk)
        maps.append(mp)
    return maps


_CACHE = {}


def kernel(**inputs):
    maps = host_inputs(inputs)
    if "nc" not in _CACHE:
        _CACHE["nc"] = build_program()[0]
    res = run_bass_kernel_spmd(_CACHE["nc"], maps, core_ids=list(range(8)))
    yp = np.zeros((32, 256, D), np.float32)
    ys = np.zeros((2, 4096, D), np.float32)
    st = np.zeros((32, NL, 2, H, 64, 128), np.float32)
    for core in range(8):
        r = res.results[core]
        b, q = core // 4, core % 4
        yp[4 * core:4 * core + 4] = np.asarray(r["yp"]).reshape(4, 256, D)
        ys[b, q * 1024:(q + 1) * 1024] = np.asarray(r["ys"])
        st[4 * core:4 * core + 4] = np.asarray(r["st"]).reshape(4, NL, 2, H, 64, 128)
    return yp, ys, st
```

```python
import numpy as np
import concourse.bass as bass
import concourse.mybir as mybir
from concourse.bass_utils import run_bass_kernel_spmd

F32 = mybir.dt.float32
BF16 = mybir.dt.bfloat16
AF = mybir.ActivationFunctionType
ALU = mybir.AluOpType

D = 2048
NL = 2
T = 512
NB = 4
KC = 16
H = 64
G = 8
EPS = 1e-6
A_B, A_C, A_H, A_Z = 0, 2048, 4096, 6144
B_Z = 8192
B_X = 12288
B_BM = 16384
B_CM = 17408
B_DT = 18432
C_A = 18560
C_G = 20608
C_Z = 22656
G_A, G_B, G_C = 24704, 26752, 28800
N_IN = 30848

V_GPRE, V_GPOST, V_CCB, V_LNG, V_LNB = 0, 16, 32, 48, 64
V_SNG = 80
V_SCB = 112
V_BMOD = 160
V_CAW = 208
V_SCW = 256
V_CCW = 496
NV = 992
NCST = 6


class Sched:
    def __init__(self, nc):
        self.nc = nc
        self.eng = {"pe": nc.tensor, "act": nc.scalar, "dve": nc.vector, "pool": nc.gpsimd, "sp": nc.sync}
        self.sem = {e: nc.alloc_semaphore(name="sem_" + e) for e in ("pe", "act", "dve", "pool")}
        self.cnt = {e: 0 for e in self.sem}
        self.seen = {e: {} for e in self.eng}
        self.ndma = 24
        self.dsem = [nc.alloc_semaphore(name="dsem%d" % i) for i in range(self.ndma)]
        self.dcnt = [0] * self.ndma
        self.drr = 0
        self.lastw = {}
        self.readers = {}
        self.nops = 0
        self.alias = {}
        self.fence = None
        self.psum_names = set()

    def key(self, ap):
        if isinstance(ap, str):
            return [ap]
        name = ap.tensor.name if hasattr(ap, "tensor") else ap.name
        return self.alias.get(name, [name])

    def _deps(self, reads, writes):
        deps = []
        for k in reads:
            if k in self.lastw:
                deps.append(self.lastw[k])
        for k in writes:
            if k in self.lastw:
                deps.append(self.lastw[k])
            for (src, val) in self.readers.get(k, ()):
                if src == "pe":
                    val = min(val + 2, self.cnt["pe"])
                deps.append((src, val))
        return deps

    paranoid = False

    def _wait(self, e, deps):
        if self.paranoid is True or (self.paranoid and e in self.paranoid):
            deps = list(deps) + [(x, self.cnt[x]) for x in ("pe", "act", "dve") if self.cnt[x] > 0]
            deps += [(s, self.dcnt[s]) for s in range(self.ndma) if self.dcnt[s] > 0]
        need = {}
        for src, val in deps:
            if src == e and e == "pe" and not self.paranoid:
                continue
            if need.get(src, 0) < val:
                need[src] = val
        for src, val in need.items():
            if self.seen[e].get(src, 0) >= val:
                continue
            if src == e and self.cnt[e] - val >= 2:
                continue
            sem = self.sem[src] if isinstance(src, str) else self.dsem[src]
            self.eng[e].wait_ge(sem, val)
            self.seen[e][src] = val

    def _commit(self, tok, reads, writes):
        for k in reads:
            self.readers.setdefault(k, []).append(tok)
        for k in writes:
            self.lastw[k] = tok
            self.readers[k] = []

    def op(self, e, outs, ins, fn):
        reads = [k for a in ins for k in self.key(a)]
        writes = [k for a in outs for k in self.key(a)]
        writes += [k for k in reads if k in self.psum_names and k not in writes]
        deps = self._deps(reads, writes)
        if e == "pe" and self.fence is not None:
            nd = []
            for (src, val) in deps:
                if src in ("act", "dve"):
                    if self.cnt[src] <= val:
                        inst = self.fence[src]()
                        self.cnt[src] += 1
                        inst.then_inc(self.sem[src], 1)
                    val = val + 1
                nd.append((src, val))
            deps = nd
        self._wait(e, deps)
        inst = fn()
        self.cnt[e] += 1
        inst.then_inc(self.sem[e], 1)
        self._commit((e, self.cnt[e]), reads, writes)
        self.nops += 1

    def dma(self, q, out, in_, extra_reads=(), track_out=True):
        reads = self.key(in_) + [k for a in extra_reads for k in self.key(a)]
        writes = self.key(out) if track_out else []
        s = self.drr
        self.drr = (self.drr + 1) % self.ndma
        deps = self._deps(reads, writes)
        if self.dcnt[s] > 0:
            deps.append((s, self.dcnt[s]))
        self._wait(q, deps)
        self.dcnt[s] += 16
        self.eng[q].dma_start(out=out, in_=in_).then_inc(self.dsem[s], 16)
        self._commit((s, self.dcnt[s]), reads, writes)
        self.nops += 1

    def collective(self, out, in_, groups):
        reads = self.key(in_)
        writes = self.key(out)
        s = self.drr
        self.drr = (self.drr + 1) % self.ndma
        deps = self._deps(reads, writes)
        if self.dcnt[s] > 0:
            deps.append((s, self.dcnt[s]))
        self._wait("pool", deps)
        self.dcnt[s] += 1
        self.nc.gpsimd.collective_compute("AllReduce", ALU.add, replica_groups=groups,
                                          ins=[in_.opt()], outs=[out.opt()]).then_inc(self.dsem[s])
        self._commit((s, self.dcnt[s]), reads, writes)

    def finish(self):
        for s in range(self.ndma):
            if self.dcnt[s] > 0:
                self.nc.sync.wait_ge(self.dsem[s], self.dcnt[s])
        for e in ("pe", "act", "dve"):
            if self.cnt[e] > 0:
                self.nc.sync.wait_ge(self.sem[e], self.cnt[e])
        for sem in list(self.sem.values()) + list(self.dsem):
            self.nc.sync.sem_clear(sem)


def build_program(cfg=None):
    cfg = cfg or {}
    layers = cfg.get("layers", [0, 1])
    do_prompt = cfg.get("prompt", True)
    do_sample = cfg.get("sample", True)
    use_cc = cfg.get("cc", True)
    ptiles = cfg.get("ptiles", [0, 1])
    br = cfg.get("br", "ACB")
    nc = bass.Bass("TRN2", target_bir_lowering=False)
    S = Sched(nc)
    S.paranoid = cfg.get('paranoid', False)
    PE, ACT, DVE = nc.tensor, nc.scalar, nc.vector

    def din(name, shape):
        return nc.dram_tensor(name, list(shape), F32, kind="ExternalInput").ap()

    xp_d = din("xp", [1024, D])
    xs_d = din("xs", [1024, D])
    h0_d = din("h0", [NL, 2, 4096, 128])
    cond_d = din("cond", [128, 2 * KC])
    vec_d = din("vec", [NL, 128, NV])
    bc_d = din("bc", [NL, 128, 320])
    cst_d = din("cst", [128, NCST * 128])
    msk_d = din("msk", [128, 16])
    w_mod_d = din("w_mod", [NL, D, 3 * D])
    w_in_d = din("w_in", [NL, D, N_IN])
    w_pa_d = din("w_pa", [NL, 2048, D])
    w_pb_d = din("w_pb", [NL, 4096, D])
    w_pc_d = din("w_pc", [NL, 2048, D])
    w_o_d = din("w_o", [NL, D, D])
    yp_d = nc.dram_tensor("yp", [1024, D], F32, kind="ExternalOutput").ap()
    ys_d = nc.dram_tensor("ys", [1024, D], F32, kind="ExternalOutput").ap()
    st_d = nc.dram_tensor("st", [4, NL, 2, 4096, 128], F32, kind="ExternalOutput").ap()

    def dscr(name, shape):
        return nc.dram_tensor(name, list(shape), F32).ap()

    yscr = {(k, t): dscr("yscr_%s%d" % (k, t), [128, KC * T]) for k in ("p", "s") for t in range(2)}
    h0t = {(l, d): dscr("h0t_%d_%d" % (l, d), [128, 4096]) for l in range(NL) for d in range(2)}
    sa_f = dscr("sa_f", [128, 4096])
    sb_t = [dscr("sb_t%d" % t, [128, 4096]) for t in range(2)]
    hinf = dscr("hinf", [128, 4096])
    hinb = dscr("hinb", [128, 4096])
    hb0 = dscr("hb0", [128, 4096])
    fcar = dscr("fcar", [128, 4096])
    zer = dscr("zer", [128, 4096])
    ccin = [dscr("ccin%d" % k, [512, 2048]) for k in range(4)] + [dscr("ccin4", [512, 128])]
    ccout = [dscr("ccout%d" % k, [512, 2048]) for k in range(4)] + [dscr("ccout4", [512, 128])]

    def sb(name, shape, dt=F32):
        return nc.alloc_sbuf_tensor("s_" + name, list(shape), dt)

    class Arena:
        def __init__(self, name):
            self.name = name
            base0 = (nc.sbuf_base + 31) // 32 * 32
            self.t = nc.alloc_sbuf_tensor(name, [128, 8192], F32)
            self.base = base0
            self.n = 0

        def view(self, off, shape, dt):
            nbytes = shape[1] * (4 if dt == F32 else 2)
            self.n += 1
            t = nc.alloc_sbuf_tensor_at("%s_v%d" % (self.name, self.n), list(shape), dt, offset=self.base + off)
            p0, p1 = off // 4096, (off + nbytes - 1) // 4096
            S.alias[t.name] = ["%s_p%d" % (self.name, p) for p in range(p0, p1 + 1)]
            return t

    ar0 = Arena("big0")
    ar1 = Arena("big1")
    big0 = [ar0.view(i * 4096, [128, 1024], F32) for i in range(8)]
    big1 = [ar1.view(i * 4096, [128, 1024], F32) for i in range(8)]
    yaT = [ar0.view(j * 1024, [128, T], BF16) for j in range(KC)]
    ycT = [ar0.view(16384 + j * 1024, [128, T], BF16) for j in range(KC)]
    mT = [ar0.view(n * 2048, [128, T], F32) for n in range(KC)]
    Rt = [ar0.view(d * 4096, [128, 1024], BF16) for d in range(2)]
    LT = [ar0.view(8192 + d * 2048, [128, 1024], BF16) for d in range(2)]
    MT = [ar0.view(12288 + d * 2048, [128, 1024], BF16) for d in range(2)]
    hprev = [[ar0.view(16384 + (b * 2 + d) * 1024, [128, 512], BF16) for d in range(2)] for b in range(NB)]
    xdt = [[ar0.view(24576 + (b * 2 + d) * 1024, [128, 512], BF16) for d in range(2)] for b in range(NB)]
    xTt = [ar1.view(k * 2048, [128, T], F32) for k in range(KC)]
    ybT = [ar1.view(c * 1024, [128, T], BF16) for c in range(32)]
    ybTg = [ar1.view(g * 4096, [128, 4 * T], BF16) for g in range(8)]

    uT = [sb("uT%d" % k, [128, T], BF16) for k in range(KC)]
    mg = [sb("mg%d" % k, [128, T], BF16) for k in range(KC)]
    NRING = 8
    ring = [sb("ring%d" % i, [128, 2048], BF16) for i in range(NRING)]
    cstf = sb("cstf", [128, NCST * 128])
    cstb = sb("cstb", [128, NCST * 128], BF16)
    vecs = sb("vecs", [128, NV])
    bcts = sb("bcts", [128, 320])
    msk = sb("msk", [128, 16])
    cond = sb("cond", [128, 2 * KC])
    scond = sb("scond", [128, 2 * KC], BF16)
    modv = sb("modv", [128, 2, 48])
    gsv = sb("gsv", [128, 2, KC])
    ggv = sb("ggv", [128, 2, KC])
    negA = sb("negA", [128, 128])
    rstd_b = sb("rstd_b", [128, T])
    tmpA = [sb("tmpA%d" % i, [128, T]) for i in range(4)]
    tmpB = [sb("tmpB%d" % i, [128, T], BF16) for i in range(4)]
    dtt = [sb("dtt%d" % b, [128, 128]) for b in range(NB)]
    at = [sb("at%d" % b, [128, 128]) for b in range(NB)]
    ecum = [sb("ecum%d" % b, [128, 128]) for b in range(NB)]
    dtd = [sb("dtd%d" % b, [128, 128]) for b in range(NB)]
    cdb = [sb("cdb%d" % b, [128, 128]) for b in range(NB)]
    tot_acc = sb("tot_acc", [128, 128])
    cumt = sb("cumt", [128, 128])
    xcT = [sb("xcT%d" % j, [128, T], BF16) for j in range(4)]
    bmT = sb("bmT", [128, T], BF16)
    cmT = sb("cmT", [128, T], BF16)
    xg = [sb("xg%d" % b, [128, 512], BF16) for b in range(NB)]
    bmg = [sb("bmg%d" % b, [128, 128], BF16) for b in range(NB)]
    szg = [sb("szg%d" % b, [128, 512], BF16) for b in range(NB)]
    xdd = [sb("xdd%d" % i, [128, 512], BF16) for i in range(2)]
    xds = sb("xds", [128, 512], BF16)
    hT = [sb("hT%d" % d, [128, 512]) for d in range(2)]
    smk = [sb("smk%d" % d, [128, 128], BF16) for d in range(2)]
    ygt = sb("ygt", [128, 512], BF16)
    sttmp = sb("sttmp", [128, 512])
    ostage = [sb("ostage%d" % i, [128, 512]) for i in range(2)]
    xstg = [ar1.view(i * 2048, [128, 512], F32) for i in range(2)]
    logD = sb("logD", [128, 2, 128])
    lgq = sb("lgq", [128, 128])
    gath = sb("gath", [128, 4, 128])
    dmt = sb("dmt", [128, 128])

    pbk = [nc.alloc_psum_tensor("pb%d" % i, [128, 512], F32) for i in range(7)]
    ptb = nc.alloc_psum_tensor("ptb", [128, 1024], BF16)
    S.psum_names = set(t.name for t in pbk) | {ptb.name}
    rot = {"i": 0, "set": [0, 1, 2, 3, 6]}

    def bank():
        rs = rot["set"]
        b = pbk[rs[rot["i"] % len(rs)]]
        rot["i"] += 1
        return b

    B4, B5 = pbk[4], pbk[5]

    fz = [sb("fz%d" % i, [128, 8]) for i in range(2)]
    if cfg.get("fence", False):
        S.fence = {"dve": lambda: DVE.memset(fz[0][:, :], 0.0), "act": lambda: ACT.activation(out=fz[1][:, 0:4], in_=fz[1][:, 4:8], func=AF.Copy)}

    def cf(i):
        return cstf[:, i * 128:(i + 1) * 128]

    def cb(i):
        return cstb[:, i * 128:(i + 1) * 128]

    ID, TRIF, TRIB, UF, UB, ONES = range(6)

    def isap(a):
        return a is not None and not isinstance(a, (int, float))

    def V(out, in0, in1, op):
        S.op("dve", [out], [in0, in1], lambda: DVE.tensor_tensor(out=out, in0=in0, in1=in1, op=op))

    def VS(out, in0, s1, s2, op0, op1=None):
        rd = [in0] + [a for a in (s1, s2) if isap(a)]
        if op1 is None:
            S.op("dve", [out], rd, lambda: DVE.tensor_scalar(out=out, in0=in0, scalar1=s1, scalar2=None, op0=op0))
        else:
            S.op("dve", [out], rd, lambda: DVE.tensor_scalar(out=out, in0=in0, scalar1=s1, scalar2=s2, op0=op0, op1=op1))

    def STT(out, in0, sc, in1, op0, op1):
        rd = [in0, in1] + ([sc] if isap(sc) else [])
        S.op("dve", [out], rd, lambda: DVE.scalar_tensor_tensor(out=out, in0=in0, scalar=sc, in1=in1, op0=op0, op1=op1))

    def VC(out, in_):
        S.op("dve", [out], [in_], lambda: DVE.tensor_copy(out=out, in_=in_))

    def VR(out, in_):
        S.op("dve", [out], [in_], lambda: DVE.reciprocal(out=out, in_=in_))

    def A(out, in_, func, bias=None, scale=None):
        rd = [in_] + [a for a in (bias, scale) if isap(a)]
        kw = {}
        if bias is not None:
            kw["bias"] = bias
        if scale is not None:
            kw["scale"] = scale
        S.op("act", [out], rd, lambda: ACT.activation(out=out, in_=in_, func=func, **kw))

    def MMG(outs, ins, mms):
        def fn():
            inst = None
            for (o, l, r, st, sp) in mms:
                inst = PE.matmul(o, lhsT=l, rhs=r, start=st, stop=sp)
            return inst
        S.op("pe", outs, ins, fn)

    def TRB(out_t, pairs):
        def fn():
            inst = None
            for (o, i) in pairs:
                inst = PE.transpose(o, i, cb(ID))
            return inst
        S.op("pe", [out_t], [p[1] for p in pairs] + [cstb], fn)

    def TRF(out_t, pairs):
        MMG([out_t], [p[1] for p in pairs] + [cstf], [(o, i, cf(ID), True, True) for (o, i) in pairs])

    def rsqrt_from(out, in_, scale, bias):
        A(out, in_, AF.Sqrt, bias=bias, scale=scale)
        VR(out, out)

    ring_i = {"i": 0}

    def wload(src):
        views = []
        for r0 in range(0, src.shape[0], 2048):
            slot = ring[ring_i["i"] % NRING]
            ring_i["i"] += 1
            view = slot[:, :].rearrange("p (k n) -> p k n", k=16)
            S.dma("pool", view, src[r0:r0 + 2048, :].rearrange("(k p) n -> p k n", p=128))
            views.append(view)
        return views

    def proj_fm(wvs, out_bank, src=None):
        src = src or uT
        kcs = 16 * len(wvs)
        mm = [(out_bank[:, :], wvs[k // 16][:, k % 16, :], src[k][:, :], k == 0, k == kcs - 1) for k in range(kcs)]
        MMG([out_bank], list(wvs) + [src[k] for k in range(kcs)], mm)

    for sem in list(S.sem.values()) + list(S.dsem):
        nc.sync.sem_clear(sem)
    nc.all_engine_barrier()
    S.dma("sp", cstf[:, :], cst_d)
    S.dma("pool", cstb[:, :], cst_d)
    S.dma("sp", msk[:, :], msk_d)
    S.dma("sp", cond[:, :], cond_d)
    for i in range(4):
        S.op("dve", [big0[i]], [], lambda i=i: DVE.memset(big0[i][:, :], 0.0))
    for i in range(4):
        S.dma("sp", zer[:, i * 1024:(i + 1) * 1024], big0[i][:, :])
    A(tmpA[0][:, 0:32], cond[:, :], AF.Silu)
    VC(scond[:, :], tmpA[0][:, 0:32])
    if do_sample:
        for l in layers:
            for d in range(2):
                for c8 in range(4):
                    stg = big0[4 + c8 % 2]
                    S.dma("sp", stg[:, :].rearrange("p (c n) -> p c n", c=8),
                          h0_d[l, d, c8 * 1024:(c8 + 1) * 1024, :].rearrange("(c p) n -> p c n", p=128))
                    for q4 in range(2):
                        bk = bank()
                        TRF(bk, [(bk[:, j * 128:(j + 1) * 128], stg[:, (q4 * 4 + j) * 128:(q4 * 4 + j + 1) * 128]) for j in range(4)])
                        tt = tmpA[q4]
                        A(tt[:, :], bk[:, :], AF.Copy)
                        col = (c8 * 8 + q4 * 4) * 128
                        S.dma("sp", h0t[(l, d)][:, col:col + 512], tt[:, :])

    def mod_phase(l):
        S.dma("sp", vecs[:, :], vec_d[l])
        S.dma("sp", bcts[:, :], bc_d[l])
        for n in range(48):
            wv = wload(w_mod_d[l][:, n * 128:(n + 1) * 128])[0]
            bk = bank()
            mm = [(bk[:, 0:2], wv[:, k, :], scond[:, k:k + KC + 1:KC], k == 0, k == KC - 1) for k in range(KC)]
            MMG([bk], [wv, scond], mm)
            for c in range(2):
                VS(modv[:, c, n:n + 1], bk[:, c:c + 1], vecs[:, V_BMOD + n:V_BMOD + n + 1], None, ALU.add)
        for c in range(2):
            STT(gsv[:, c, :], modv[:, c, 16:32], 1.0, vecs[:, V_GPRE:V_GPRE + 16], ALU.add, ALU.mult)
            V(ggv[:, c, :], modv[:, c, 32:48], vecs[:, V_GPOST:V_GPOST + 16], ALU.mult)
        A(negA[:, :], bcts[:, 128:256], AF.Exp)
        VS(negA[:, :], negA[:, :], -1.0, None, ALU.mult)

    def load_x_tokmajor(src_rows):
        for b in range(NB):
            for hh in range(2):
                S.dma("sp", big0[b * 2 + hh][:, :], src_rows[b * 128:(b + 1) * 128, hh * 1024:(hh + 1) * 1024])
        for k in range(KC):
            bk = bank()
            TRF(bk, [(bk[:, b * 128:(b + 1) * 128], big0[b * 2 + k // 8][:, (k % 8) * 128:(k % 8 + 1) * 128]) for b in range(NB)])
            A(xTt[k][:, :], bk[:, :], AF.Copy)

    def load_x_featmajor(scr):
        for i in range(8):
            S.dma("sp", big1[i][:, :], scr[:, i * 1024:(i + 1) * 1024])

    def rms_bcast(srcs, denom, out_rstd):
        n = len(srcs)
        for k in range(n):
            sq = tmpB[k % 4]
            A(sq[:, :], srcs[k][:, :], AF.Square)
            MMG([B4], [cstb, sq], [(B4[:, :], cb(ONES), sq[:, :], k == 0, k == n - 1)])
        rsqrt_from(out_rstd[:, :], B4[:, :], 1.0 / denom, EPS)

    def make_u(c):
        rms_bcast(xTt, float(D), rstd_b)
        for k in range(KC):
            t = tmpA[k % 4]
            V(t[:, :], xTt[k][:, :], rstd_b[:, :], ALU.mult)
            A(uT[k][:, :], t[:, :], AF.Identity, bias=modv[:, c, k:k + 1], scale=gsv[:, c, k:k + 1])

    cacc = None
    dgw = sb("dgw", [128, 31 * 128], BF16)
    gpad = sb("gpad", [128, 752], BF16)

    def conv_fm(out, src, wcol, K, seg, bias=None):
        c = K // 2
        if K > 8:
            return conv_fm2(out, src, wcol, K, seg, bias)
        if bias is None:
            VS(out, src, wcol(c), None, ALU.mult)
        else:
            A(out, src, AF.Identity, bias=bias, scale=wcol(c))
        ov = out.rearrange("p (s t) -> p s t", t=seg)
        sv = src.rearrange("p (s t) -> p s t", t=seg)
        for k in range(K):
            o = k - c
            if o == 0 or abs(o) >= seg:
                continue
            if o > 0:
                STT(ov[:, :, 0:seg - o], sv[:, :, o:seg], wcol(k), ov[:, :, 0:seg - o], ALU.mult, ALU.add)
            else:
                STT(ov[:, :, -o:seg], sv[:, :, 0:seg + o], wcol(k), ov[:, :, -o:seg], ALU.mult, ALU.add)

    def conv_fm2(out, src, wcol, K, seg, bias):
        c = K // 2
        VS(out, src, wcol(c), bias, ALU.mult, ALU.add)
        acc2 = cacc[:, :]
        S.op("dve", [cacc], [], lambda: DVE.memset(cacc[:, :], 0.0))
        accs = [out, acc2]
        sv = src.rearrange("p (s t) -> p s t", t=seg)
        i = 0
        for k in range(K):
            o = k - c
            if o == 0 or abs(o) >= seg:
                continue
            i += 1
            ov = accs[i % 2].rearrange("p (s t) -> p s t", t=seg)
            if o > 0:
                STT(ov[:, :, 0:seg - o], sv[:, :, o:seg], wcol(k), ov[:, :, 0:seg - o], ALU.mult, ALU.add)
            else:
                STT(ov[:, :, -o:seg], sv[:, :, 0:seg + o], wcol(k), ov[:, :, -o:seg], ALU.mult, ALU.add)
        V(out, out, acc2, ALU.add)

    def out_proj(l, wd, src, kcs, gate_c0, first, post=None):
        for n in range(KC):
            if True:
                wg = wload(w_in_d[l][:, gate_c0 + n * 128:gate_c0 + (n + 1) * 128])
                wv = wload(wd[l][:, n * 128:(n + 1) * 128])
                bg = bank()
                proj_fm(wg, bg)
                gt = tmpA[n % 2]
                A(gt[:, :], bg[:, :], AF.Sigmoid)
                bk = bank()
                proj_fm(wv, bk, src=src)
                if post is not None:
                    V(gt[:, :], gt[:, :], post[:, :], ALU.mult)
                if first:
                    V(mg[n][:, :], bk[:, :], gt[:, :], ALU.mult)
                else:
                    t2 = tmpA[2 + n % 2]
                    V(t2[:, :], bk[:, :], gt[:, :], ALU.mult)
                    V(mg[n][:, :], mg[n][:, :], t2[:, :], ALU.add)

    def vcol(off):
        return vecs[:, off:off + 1]

    def branch_A(l, seg):
        for j in range(KC):
            if True:
                h2 = j % 2
                wb = wload(w_in_d[l][:, A_B + j * 128:A_B + (j + 1) * 128])
                wc = wload(w_in_d[l][:, A_C + j * 128:A_C + (j + 1) * 128])
                wh = wload(w_in_d[l][:, A_H + j * 128:A_H + (j + 1) * 128])
                wz = wload(w_in_d[l][:, A_Z + j * 128:A_Z + (j + 1) * 128])
                pb_, pc_ = bank(), bank()
                proj_fm(wb, pb_)
                proj_fm(wc, pc_)
                tb, tc = tmpA[h2 * 2], tmpA[h2 * 2 + 1]
                A(tb[:, :], pb_[:, :], AF.Copy)
                A(tc[:, :], pc_[:, :], AF.Copy)
                ph_, pz_ = bank(), bank()
                proj_fm(wh, ph_)
                proj_fm(wz, pz_)
                V(tc[:, :], ph_[:, :], tc[:, :], ALU.mult)
                cv = sttmp
                conv_fm(cv[:, :], tc[:, :], lambda k, j=j: vcol(V_CAW + j * 3 + k), 3, seg)
                V(cv[:, :], cv[:, :], tb[:, :], ALU.mult)
                sz = tmpB[h2]
                A(sz[:, :], pz_[:, :], AF.Silu)
                V(yaT[j][:, :], cv[:, :], sz[:, :], ALU.mult)
        out_proj(l, w_pa_d, yaT, KC, G_A, True)

    def branch_C(l, seg):
        S.op("dve", [gpad], [], lambda: DVE.memset(gpad[:, :], 0.0))
        for j in range(KC):
            if True:
                h2 = j % 2
                wa = wload(w_in_d[l][:, C_A + j * 128:C_A + (j + 1) * 128])
                wg = wload(w_in_d[l][:, C_G + j * 128:C_G + (j + 1) * 128])
                pa_, pg_ = bank(), bank()
                proj_fm(wa, pa_)
                proj_fm(wg, pg_)
                sg = tmpA[h2]
                A(sg[:, :], pg_[:, :], AF.Sigmoid)
                gw = seg + 30
                nseg = T // seg
                nh = nseg // 2
                Wh = nh * gw
                Nh = Wh - 30
                gv = gpad[:, 0:nseg * gw].rearrange("p (s w) -> p s w", w=gw)
                V(gv[:, :, 15:15 + seg], pa_[:, :].rearrange("p (s t) -> p s t", t=seg),
                  sg[:, :].rearrange("p (s t) -> p s t", t=seg), ALU.mult)
                V(dgw[:, :].rearrange("p (k n) -> p k n", k=31), cb(ID).unsqueeze(1).to_broadcast([128, 31, 128]),
                  vecs[:, V_CCW + j * 31:V_CCW + (j + 1) * 31].unsqueeze(2).to_broadcast([128, 31, 128]), ALU.mult)
                sq = tmpB[2 + h2]
                for hf in range(2):
                    cvb = bank()
                    mm = [(cvb[:, 0:Nh], dgw[:, k * 128:(k + 1) * 128], gpad[:, hf * Wh + k:hf * Wh + k + Nh], k == 0, k == 30)
                          for k in range(31)]
                    MMG([cvb], [dgw, gpad], mm)
                    cvv = cvb[:, 0:Wh].rearrange("p (s w) -> p s w", w=gw)[:, :, 0:seg]
                    yv_ = ycT[j][:, hf * 256:(hf + 1) * 256].rearrange("p (s t) -> p s t", t=seg)
                    sv_ = sq[:, hf * 256:(hf + 1) * 256].rearrange("p (s t) -> p s t", t=seg)
                    A(yv_, cvv, AF.Identity, bias=vcol(V_CCB + j))
                    A(sv_, cvv, AF.Square, bias=vcol(V_CCB + j))
                MMG([B4], [cstb, ycT[j]], [(B4[:, :], cb(ONES), ycT[j][:, :], j == 0, j == KC - 1)])
                MMG([B5], [cstb, sq], [(B5[:, :], cb(ONES), sq[:, :], j == 0, j == KC - 1)])
        mean, rs = tmpA[2], tmpA[3]
        VS(mean[:, :], B4[:, :], 1.0 / 2048, None, ALU.mult)
        V(rs[:, :], mean[:, :], mean[:, :], ALU.mult)
        STT(rs[:, :], B5[:, :], 1.0 / 2048, rs[:, :], ALU.mult, ALU.subtract)
        rsqrt_from(rs[:, :], rs[:, :], 1.0, EPS)
        V(mean[:, :], mean[:, :], rs[:, :], ALU.mult)
        for j in range(KC):
            if True:
                h2 = j % 2
                wz = wload(w_in_d[l][:, C_Z + j * 128:C_Z + (j + 1) * 128])
                pz_ = bank()
                proj_fm(wz, pz_)
                t = tmpA[h2]
                V(t[:, :], ycT[j][:, :], rs[:, :], ALU.mult)
                V(t[:, :], t[:, :], mean[:, :], ALU.subtract)
                A(t[:, :], t[:, :], AF.Silu, bias=vcol(V_LNB + j), scale=vcol(V_LNG + j))
                sz = tmpB[h2]
                A(sz[:, :], pz_[:, :], AF.Silu)
                V(ycT[j][:, :], t[:, :], sz[:, :], ALU.mult)
        out_proj(l, w_pc_d, ycT, KC, G_C, br[0] == 'C')

    def ssd_dt(l, full):
        wv = wload(w_in_d[l][:, B_DT:B_DT + 128])[0]
        for b in range(NB):
            bk = bank()
            mm = [(bk[:, 0:128], uT[k][:, b * 128:(b + 1) * 128], wv[:, k, :], k == 0, k == KC - 1) for k in range(KC)]
            MMG([bk], [wv] + uT, mm)
            V(dtt[b][:, :], bk[:, 0:128], bcts[:, 0:128], ALU.add)
            A(dtt[b][:, :], dtt[b][:, :], AF.Exp)
            A(dtt[b][:, :], dtt[b][:, :], AF.Ln, bias=1.0)
            V(at[b][:, :], dtt[b][:, :], negA[:, :], ALU.mult)
            bk2 = bank()
            MMG([bk2], [cstf, at[b]], [
                (bk2[:, 0:64], cf(TRIF), at[b][:, 0:64], True, True),
                (bk2[:, 64:128], cf(TRIB), at[b][:, 64:128], True, True),
                (bk2[:, 128:256], cf(ONES), at[b][:, :], True, True)])
            A(cumt[:, :], bk2[:, 0:128], AF.Copy)
            if full:
                A(ecum[b][:, :], bk2[:, 0:128], AF.Exp)
            V(dtd[b][:, :], bk2[:, 128:256], cumt[:, :], ALU.subtract)
            A(dtd[b][:, :], dtd[b][:, :], AF.Exp)
            V(dtd[b][:, :], dtd[b][:, :], dtt[b][:, :], ALU.mult)
            A(cdb[b][:, :], bk2[:, 128:256], AF.Exp)
            if b == 0:
                A(tot_acc[:, :], bk2[:, 128:256], AF.Copy)
            else:
                V(tot_acc[:, :], tot_acc[:, :], bk2[:, 128:256], ALU.add)

    def hsl(d, g):
        return slice(d * 64 + g * 8, d * 64 + g * 8 + 8)

    def bc8(ap):
        return ap.unsqueeze(2).to_broadcast([128, 8, 64])

    def v3(ap):
        return ap.rearrange("p (h q) -> p h q", h=8)

    def ssd_group(l, g, full, seg, kind, init_src, end_dst, st_out):
        for idx in range(6):
            if idx == 4:
                it = (wload(w_in_d[l][:, B_BM + g * 128:B_BM + (g + 1) * 128]), 32 + g, bmT)
            elif idx == 5:
                if not full:
                    continue
                it = (wload(w_in_d[l][:, B_CM + g * 128:B_CM + (g + 1) * 128]), 40 + g, cmT)
            else:
                it = (wload(w_in_d[l][:, B_X + g * 512 + idx * 128:B_X + g * 512 + (idx + 1) * 128]), g * 4 + idx, xcT[idx])
            (wv, ci, dst) = it
            bk = bank()
            proj_fm(wv, bk)
            cv = sttmp
            conv_fm(cv[:, :], bk[:, :], lambda k, ci=ci: vcol(V_SCW + ci * 5 + k), 5, seg, bias=vcol(V_SCB + ci))
            A(dst[:, :], cv[:, :], AF.Silu)
        for b in range(NB):
            prs = [(ptb[:, j * 128:(j + 1) * 128], xcT[j][:, b * 128:(b + 1) * 128]) for j in range(4)]
            prs.append((ptb[:, 512:640], bmT[:, b * 128:(b + 1) * 128]))
            TRB(ptb, prs)
            A(xg[b][:, :], ptb[:, 0:512], AF.Copy)
            VC(bmg[b][:, :], ptb[:, 512:640])
        if full:
            wzs = [wload(w_in_d[l][:, B_Z + g * 512 + i * 128:B_Z + g * 512 + (i + 1) * 128])[0] for i in range(4)]
            for b in range(NB):
                bk = bank()
                mm = [(bk[:, i * 128:(i + 1) * 128], uT[k][:, b * 128:(b + 1) * 128], wzs[i][:, k, :], k == 0, k == KC - 1)
                      for i in range(4) for k in range(KC)]
                MMG([bk], wzs + uT, mm)
                A(szg[b][:, :], bk[:, :], AF.Silu)
        for b in range(NB if full else 0):
            for d in range(2):
                V(v3(xdt[b][d][:, :]), v3(xg[b][:, :]), bc8(dtt[b][:, hsl(d, g)]), ALU.mult)
        orders = [list(range(NB)), list(range(NB - 1, -1, -1))]
        if kind == "s":
            for d in range(2):
                S.dma("sp", hT[d][:, :], init_src[d][:, g * 512:(g + 1) * 512])
        for pos in range(NB):
            for d in range(2):
                b = orders[d][pos]
                seq_first = (kind == "p" and pos % 2 == 0)
                seq_last = (kind == "p" and pos % 2 == 1)
                xd = xdd[d]
                V(v3(xd[:, :]), v3(xg[b][:, :]), bc8(dtd[b][:, hsl(d, g)]), ALU.mult)
                if full and not seq_first:
                    A(hprev[b][d][:, :], hT[d][:, :], AF.Copy)
                bk = bank()
                MMG([bk], [bmg[b], xd], [(bk[:, :], bmg[b][:, :], xd[:, :], True, True)])
                if seq_first:
                    A(hT[d][:, :], bk[:, :], AF.Copy)
                else:
                    V(v3(hT[d][:, :]), v3(hT[d][:, :]), bc8(cdb[b][:, hsl(d, g)]), ALU.mult)
                    V(hT[d][:, :], hT[d][:, :], bk[:, :], ALU.add)
                if seq_last:
                    seq = b // 2
                    bk2 = bank()
                    TRF(bk2, [(bk2[:, j * 128:(j + 1) * 128], hT[d][:, j * 128:(j + 1) * 128]) for j in range(4)])
                    so = ostage[d]
                    A(so[:, :], bk2[:, :], AF.Copy)
                    S.dma("sp", st_out(seq, d)[g * 512:(g + 1) * 512, :].rearrange("(j p) n -> p j n", p=128),
                          so[:, :].rearrange("p (j n) -> p j n", j=4), track_out=False)
        for d in range(2):
            if kind == "s" and end_dst is not None and end_dst[d] is not None:
                S.dma("sp", end_dst[d][:, g * 512:(g + 1) * 512], hT[d][:, :])
        if not full:
            return
        YB = [B5, pbk[6]]

        def y_front(b):
            tok = slice(b * 128, (b + 1) * 128)
            Y = YB[b % 2]
            V(v3(xds[:, :]), v3(xg[b][:, :]), bc8(bcts[:, 256 + g * 8:256 + g * 8 + 8]), ALU.mult)
            MMG([B4], [bmT, cmT], [(B4[:, 0:128], bmT[:, tok], cmT[:, tok], True, True)])
            V(smk[0][:, :], B4[:, 0:128], cf(TRIF), ALU.mult)
            V(smk[1][:, :], B4[:, 0:128], cf(TRIB), ALU.mult)
            MMG([Y], [cstb, xds], [(Y[:, :], cb(ID), xds[:, :], True, False)])
            for d in range(2):
                tri = cb(TRIF) if d == 0 else cb(TRIB)
                V(Rt[d][:, :].rearrange("p (h i) -> p h i", h=8),
                  at[b][:, hsl(d, g)].unsqueeze(2).to_broadcast([128, 8, 128]),
                  tri.unsqueeze(1).to_broadcast([128, 8, 128]), ALU.mult)
                U = cb(UF) if d == 0 else cb(UB)
                for hf in range(2):
                    bk = bank()
                    MMG([bk], [cstb, Rt[d]], [(bk[:, :], U, Rt[d][:, hf * 512:(hf + 1) * 512], True, True)])
                    A(LT[d][:, hf * 512:(hf + 1) * 512], bk[:, :], AF.Exp)
                V(MT[d][:, :].rearrange("p (h i) -> p h i", h=8), LT[d][:, :].rearrange("p (h i) -> p h i", h=8),
                  smk[d][:, :].unsqueeze(1).to_broadcast([128, 8, 128]), ALU.mult)
                mm = [(Y[:, h * 64:(h + 1) * 64], MT[d][:, h * 128:(h + 1) * 128], xdt[b][d][:, h * 64:(h + 1) * 64],
                       False, (d == 1 and h == 7)) for h in range(8)]
                MMG([Y], [MT[d], xdt[b][d]], mm)

        def y_back(b):
            tok = slice(b * 128, (b + 1) * 128)
            Y = YB[b % 2]
            yo = []
            for d in range(2):
                first_blk = (kind == "p" and ((d == 0 and b % 2 == 0) or (d == 1 and b % 2 == 1)))
                if not first_blk:
                    bk = bank()
                    MMG([bk], [cmT, hprev[b][d]], [(bk[:, :], cmT[:, tok], hprev[b][d][:, :], True, True)])
                    yo.append((bk, d))
            yv = tmpA[b % 2]
            prev = Y
            for (bk, d) in yo:
                t = tmpA[2 + d]
                V(v3(t[:, :]), v3(bk[:, :]), bc8(ecum[b][:, hsl(d, g)]), ALU.mult)
                V(yv[:, :], prev[:, :], t[:, :], ALU.add)
                prev = yv
            V(ygt[:, :], prev[:, :], szg[b][:, :], ALU.mult)
            TRB(ptb, [(ptb[:, j * 128:(j + 1) * 128], ygt[:, j * 128:(j + 1) * 128]) for j in range(4)])
            A(ybTg[g][:, :].rearrange("p (j t) -> p j t", j=4)[:, :, tok],
              ptb[:, 0:512].rearrange("p (j t) -> p j t", j=4), AF.Copy)

        rot["set"] = [0, 1, 2, 3]
        y_front(0)
        for b in range(NB):
            if b + 1 < NB:
                y_front(b + 1)
            y_back(b)
        rot["set"] = [0, 1, 2, 3, 6]

    def branch_B_tail(l):
        rsb = rstd_b
        rms_bcast(ybT, 4096.0, rsb)
        for c in range(32):
            VS(ybT[c][:, :], ybT[c][:, :], vcol(V_SNG + c), None, ALU.mult)
        out_proj(l, w_pb_d, ybT, 32, G_B, br[0] == 'B', post=rsb)

    def final(l, c, first_layer, src_rows, scr_in, write_out):
        for n in range(KC):
            if True:
                wv = wload(w_o_d[l][:, n * 128:(n + 1) * 128])
                bk = bank()
                proj_fm(wv, bk, src=mg)
                A(mT[n][:, :], bk[:, :], AF.Copy)
                sq = tmpB[n % 4]
                A(sq[:, :], bk[:, :], AF.Square)
                MMG([B4], [cstb, sq], [(B4[:, :], cb(ONES), sq[:, :], n == 0, n == KC - 1)])
        rsqrt_from(rstd_b[:, :], B4[:, :], 1.0 / D, EPS)
        if not first_layer:
            load_x_featmajor(scr_in)
        for n in range(KC):
            if first_layer:
                xs_ = xstg[n % 2]
                S.dma("sp", xs_[:, :].rearrange("p (b f) -> p b f", b=NB),
                      src_rows[:, n * 128:(n + 1) * 128].rearrange("(b p) f -> p b f", p=128))
                xb_ = bank()
                TRF(xb_, [(xb_[:, b * 128:(b + 1) * 128], xs_[:, b * 128:(b + 1) * 128]) for b in range(NB)])
                xin = xb_[:, :]
            else:
                xin = xTt[n][:, :]
            t = tmpA[n % 4]
            V(t[:, :], mT[n][:, :], rstd_b[:, :], ALU.mult)
            STT(t[:, :], t[:, :], ggv[:, c, n:n + 1], xin, ALU.mult, ALU.add)
            write_out(n, t)

    def writer_scratch(scr):
        def w(n, t):
            S.dma("sp", scr[:, n * T:(n + 1) * T], t[:, :])
        return w

    def writer_tokmajor(dst_rows):
        def w(n, t):
            bk = bank()
            TRF(bk, [(bk[:, b * 128:(b + 1) * 128], t[:, b * 128:(b + 1) * 128]) for b in range(NB)])
            so = ostage[n % 2]
            A(so[:, :], bk[:, :], AF.Copy)
            S.dma("sp", dst_rows[:, n * 128:(n + 1) * 128].rearrange("(b p) f -> p b f", p=128),
                  so[:, :].rearrange("p (b f) -> p b f", b=NB), track_out=False)
        return w

    def tile_pass(l, kind, t, full, init_src=None, end_dst=None):
        c = 0 if kind == "p" else 1
        seg = 256 if kind == "p" else 64
        src_rows = (xp_d if kind == "p" else xs_d)[t * T:(t + 1) * T, :]
        first_layer = (l == layers[0])
        last_layer = (l == layers[-1])
        if first_layer:
            load_x_tokmajor(src_rows)
        else:
            load_x_featmajor(yscr[(kind, t)])
        make_u(c)
        if full and "A" in br:
            branch_A(l, seg)
        if full and "C" in br:
            branch_C(l, seg)
        ssd_dt(l, full)
        for g in range(G):
            ssd_group(l, g, full, seg, kind, init_src, end_dst,
                      (lambda seq, d: st_d[t * 2 + seq, l, d]) if kind == "p" else None)
        if not full:
            return
        if "B" in br:
            branch_B_tail(l)
        if last_layer:
            dst = (yp_d if kind == "p" else ys_d)[t * T:(t + 1) * T, :]
            final(l, c, first_layer, src_rows, yscr[(kind, t)], writer_tokmajor(dst))
        else:
            final(l, c, first_layer, src_rows, yscr[(kind, t)], writer_scratch(yscr[(kind, t)]))

    GROUPS = [[0, 1, 2, 3], [4, 5, 6, 7]]

    def ld4(pieces, src):
        for i in range(4):
            S.dma("sp", pieces[i][:, :], src[:, i * 1024:(i + 1) * 1024])

    def st4(dst, pieces):
        for i in range(4):
            S.dma("sp", dst[:, i * 1024:(i + 1) * 1024], pieces[i][:, :])

    def bc64(ap):
        return ap.unsqueeze(2).to_broadcast([128, 16, 64])

    def p3(piece):
        return piece[:, :].rearrange("p (h q) -> p h q", h=16)

    def exchange(l):
        Sb, Tm = big0[0:4], big0[4:8]
        ld4(Sb, sb_t[1])
        ld4(Tm, sb_t[0])
        A(dmt[:, :], logD[:, 0, :], AF.Exp)
        for i in range(4):
            V(p3(Sb[i]), p3(Sb[i]), bc64(dmt[:, 64 + 16 * i:64 + 16 * i + 16]), ALU.mult)
            V(Sb[i][:, :], Sb[i][:, :], Tm[i][:, :], ALU.add)
        Sf = big1[0:4]
        ld4(Sf, sa_f)
        V(lgq[:, :], logD[:, 0, :], logD[:, 1, :], ALU.add)
        Tm2 = big1[4:8]
        for r in range(4):
            for (k, src) in ((0, Sf[0:2]), (1, Sf[2:4]), (2, Sb[0:2]), (3, Sb[2:4])):
                for i in range(2):
                    tt = Tm2[(k * 2 + i) % 4]
                    VS(tt[:, :], src[i][:, :], msk[:, r:r + 1], None, ALU.mult)
                    S.dma("sp", ccin[k][r * 128:(r + 1) * 128, i * 1024:(i + 1) * 1024], tt[:, :])
            VS(tmpA[r][:, 0:128], lgq[:, :], msk[:, r:r + 1], None, ALU.mult)
            S.dma("sp", ccin[4][r * 128:(r + 1) * 128, :], tmpA[r][:, 0:128])
        for k in range(5):
            if use_cc:
                S.collective(ccout[k], ccin[k], GROUPS)
            else:
                for r in range(4):
                    S.dma("sp", ccout[k][r * 128:(r + 1) * 128, :], ccin[k][r * 128:(r + 1) * 128, :])

    def fold(l):
        for r in range(4):
            S.dma("sp", gath[:, r, :], ccout[4][r * 128:(r + 1) * 128, :])
        run, Sr = big0[0:4], big0[4:8]
        for d in range(2):
            ld4(run, h0t[(l, d)])
            steps = [0, 1, 2] if d == 0 else [3, 2, 1]
            for si, r in enumerate(steps):
                mcol = msk[:, 4 + d * 3 + si:4 + d * 3 + si + 1]
                A(dmt[:, :], gath[:, r, :], AF.Exp, scale=mcol)
                for i in range(4):
                    S.dma("sp", Sr[i][:, :], ccout[d * 2 + i // 2][r * 128:(r + 1) * 128, (i % 2) * 1024:(i % 2 + 1) * 1024])
                for i in range(4):
                    V(p3(run[i]), p3(run[i]), bc64(dmt[:, d * 64 + 16 * i:d * 64 + 16 * i + 16]), ALU.mult)
                    STT(run[i][:, :], Sr[i][:, :], mcol, run[i][:, :], ALU.mult, ALU.add)
            st4(hinf if d == 0 else hinb, run)
        A(dmt[:, :], logD[:, 1, :], AF.Exp)
        ld4(Sr, sb_t[1])
        for i in range(4):
            V(p3(run[i]), p3(run[i]), bc64(dmt[:, 64 + 16 * i:64 + 16 * i + 16]), ALU.mult)
            V(run[i][:, :], run[i][:, :], Sr[i][:, :], ALU.add)
        st4(hb0, run)

    for l in layers:
        mod_phase(l)
        if do_sample:
            for t in range(2):
                tile_pass(l, "s", t, False, init_src=[zer if t == 0 else sa_f, zer], end_dst=[sa_f, sb_t[t]])
                VC(logD[:, t, :], tot_acc[:, :])
            exchange(l)
        if do_prompt:
            for t in ptiles:
                tile_pass(l, "p", t, True)
        if do_sample:
            fold(l)
            tile_pass(l, "s", 0, True, init_src=[hinf, hb0], end_dst=[fcar, None])
            tile_pass(l, "s", 1, True, init_src=[fcar, hinb], end_dst=[None, None])
    S.finish()
    return nc, S


def _fm(v):
    return np.ascontiguousarray(v.reshape(-1, 128).T)


def host_inputs(inp):
    f = lambda a: np.ascontiguousarray(np.asarray(a, dtype=np.float32))
    x_prompt, x_sample, state = f(inp["x_prompt"]), f(inp["x_sample"]), f(inp["state_ssd"])
    c, c_ctx = f(inp["c"]), f(inp["c_ctx"])
    vec = np.zeros((NL, 128, NV), np.float32)
    bc = np.zeros((NL, 128, 320), np.float32)
    for l in range(NL):
        vec[l, :, V_GPRE:V_GPRE + 16] = _fm(f(inp["g_pre"])[l])
        vec[l, :, V_GPOST:V_GPOST + 16] = _fm(f(inp["g_post"])[l])
        vec[l, :, V_CCB:V_CCB + 16] = _fm(f(inp["conf_conv_b"])[l])
        vec[l, :, V_LNG:V_LNG + 16] = _fm(f(inp["conf_ln_g"])[l])
        vec[l, :, V_LNB:V_LNB + 16] = _fm(f(inp["conf_ln_b"])[l])
        vec[l, :, V_SNG:V_SNG + 32] = _fm(f(inp["ssd_norm_g"])[l])
        vec[l, :, V_SCB:V_SCB + 48] = _fm(f(inp["ssd_conv_b"])[l])
        vec[l, :, V_BMOD:V_BMOD + 48] = _fm(f(inp["b_mod"])[l])
        caw = f(inp["conv_a_w"])[l]
        vec[l, :, V_CAW:V_CAW + 48] = caw.T.reshape(16, 128, 3).transpose(1, 0, 2).reshape(128, 48)
        scw = f(inp["ssd_conv_w"])[l]
        vec[l, :, V_SCW:V_SCW + 240] = scw.T.reshape(48, 128, 5).transpose(1, 0, 2).reshape(128, 240)
        ccw = f(inp["conf_conv_w"])[l]
        vec[l, :, V_CCW:V_CCW + 496] = ccw.T.reshape(16, 128, 31).transpose(1, 0, 2).reshape(128, 496)
        bc[l, :, 0:128] = f(inp["dt_bias"])[l].reshape(1, 128)
        bc[l, :, 128:256] = f(inp["a_log"])[l].reshape(1, 128)
        bc[l, :, 256:320] = f(inp["d_skip"])[l].reshape(1, 64)
    i = np.arange(128)
    m, j = i[:, None], i[None, :]
    cst = np.concatenate([(m == j), (m <= j), (m >= j), (m > j), (m < j), np.ones((128, 128), bool)], axis=1).astype(np.float32)
    shared = dict(vec=vec, bc=bc, cst=cst, w_mod=f(inp["w_mod"]), w_in=f(inp["w_in"]), w_pa=f(inp["w_pa"]),
                  w_pb=f(inp["w_pb"]), w_pc=f(inp["w_pc"]), w_o=f(inp["w_o"]))
    maps = []
    for core in range(8):
        b, q = core // 4, core % 4
        msk = np.zeros((128, 16), np.float32)
        msk[:, q] = 1.0
        for si, r in enumerate([0, 1, 2]):
            msk[:, 4 + si] = 1.0 if r < q else 0.0
        for si, r in enumerate([3, 2, 1]):
            msk[:, 7 + si] = 1.0 if r > q else 0.0
        cond = np.concatenate([_fm(c_ctx), _fm(c[b])], axis=1)
        mp = dict(shared)
        mp.update(xp=np.ascontiguousarray(x_prompt[4 * core:4 * core + 4].reshape(1024, D)),
                  xs=np.ascontiguousarray(x_sample[b, q * 1024:(q + 1) * 1024]),
                  h0=np.ascontiguousarray(state[b].reshape(NL, 2, 4096, 128)),
                  cond=np.ascontiguousarray(cond), msk=msk)
        maps.append(mp)
    return maps


_CACHE = {}


def kernel(**inputs):
    maps = host_inputs(inputs)
    if "nc" not in _CACHE:
        _CACHE["nc"] = build_program()[0]
    res = run_bass_kernel_spmd(_CACHE["nc"], maps, core_ids=list(range(8)))
    yp = np.zeros((32, 256, D), np.float32)
    ys = np.zeros((2, 4096, D), np.float32)
    st = np.zeros((32, NL, 2, H, 64, 128), np.float32)
    for core in range(8):
        r = res.results[core]
        b, q = core // 4, core % 4
        yp[4 * core:4 * core + 4] = np.asarray(r["yp"]).reshape(4, 256, D)
        ys[b, q * 1024:(q + 1) * 1024] = np.asarray(r["ys"])
        st[4 * core:4 * core + 4] = np.asarray(r["st"]).reshape(4, NL, 2, H, 64, 128)
    return yp, ys, st
```
